# Optimizing a Trainium2 kernel written in Bass

```python
import math
import jax, jax.numpy as jnp
from jax import lax
import numpy as np


D_MODEL = 1024
BATCH = 8
SEQ = 8192
DEPTH = 4
DEC_BATCH = 32
DEC_SEQ = 2048
PAST_LEN = 128

HEAD_DIM = 64
A_PATTERNS = ((128, 1), (512, 4), (2048, 16))
A_GROUPS = len(A_PATTERNS)
A_HEADS_PER_GROUP = 4
A_HEADS = A_GROUPS * A_HEADS_PER_GROUP
BAND_BLOCK = 64
B_HEADS = 4
C_Q_HEADS = 16
C_KV_HEADS = 4
C_GROUP = C_Q_HEADS // C_KV_HEADS
QUERY_BLOCK = 128
ROPE_THETA = 500000.0
ROT_DIM = HEAD_DIM // 4
AXIAL_THETA = 10000.0
AXIAL_DIM = HEAD_DIM // 2
GRID_W = 64
D_FF = 2816
CONV_W = 3
NORM_EPS = 1e-6
SUBLN_EPS = 1e-5
NEG_INF = -1e30
N_EVEN = (DEPTH + 1) // 2
N_ODD = DEPTH // 2

A_QKV = A_HEADS * HEAD_DIM
B_QK = B_HEADS * 2 * HEAD_DIM
B_V = B_HEADS * 2 * HEAD_DIM
AB_IN = 3 * A_QKV + 2 * B_QK + B_V
AB_OUT = A_HEADS_PER_GROUP * HEAD_DIM + B_V
C_IN = (C_Q_HEADS + 2 * C_KV_HEADS) * HEAD_DIM
C_OUT = C_Q_HEADS * HEAD_DIM

kernel_name = 'hybrid_dilated_diff_axial_encoder'


def rms_norm(x, g, eps=NORM_EPS):
    xf = x.astype(jnp.float32)
    y = xf * lax.rsqrt(jnp.mean(xf * xf, axis=-1, keepdims=True) + eps)
    return (y * g.astype(jnp.float32)).astype(x.dtype)


def rope_cos_sin(pos, dim, theta):
    inv = theta ** (-jnp.arange(0, dim, 2, dtype=jnp.float32) / dim)
    ang = pos.astype(jnp.float32)[:, None] * inv[None, :]
    return jnp.cos(ang), jnp.sin(ang)


def rotate(x, cos, sin):
    half = x.shape[-1] // 2
    shape = cos.shape[:1] + (1,) * (x.ndim - 3) + cos.shape[1:]
    c, s = cos.reshape(shape), sin.reshape(shape)
    xf = x.astype(jnp.float32)
    x1, x2 = xf[..., :half], xf[..., half:]
    return jnp.concatenate([x1 * c - x2 * s, x2 * c + x1 * s], axis=-1).astype(x.dtype)


def partial_rotary(x, cos, sin):
    return jnp.concatenate([rotate(x[..., :ROT_DIM], cos, sin), x[..., ROT_DIM:]], axis=-1)


def axial_rotary(x, cos_r, sin_r, cos_c, sin_c):
    return jnp.concatenate([rotate(x[..., :AXIAL_DIM], cos_r, sin_r),
                            rotate(x[..., AXIAL_DIM:], cos_c, sin_c)], axis=-1)


def band_attention(q, k, v, half):
    n, L, h, dh = q.shape
    qb_size = math.gcd(L, BAND_BLOCK)
    nb = L // qb_size
    width = qb_size + 2 * half
    pad = ((0, 0), (half, half), (0, 0), (0, 0))
    kidx = jnp.arange(nb)[:, None] * qb_size + jnp.arange(width)[None, :]
    kb = jnp.pad(k, pad)[:, kidx]
    vb = jnp.pad(v, pad)[:, kidx]
    qb = q.reshape(n, nb, qb_size, h, dh)
    s = jnp.einsum('nbqhd,nbkhd->nbhqk', qb, kb).astype(jnp.float32) * (dh ** -0.5)
    rel = jnp.arange(width)[None, :] - half - jnp.arange(qb_size)[:, None]
    kpos = kidx - half
    valid = (jnp.abs(rel) <= half)[None] & ((kpos >= 0) & (kpos < L))[:, None, :]
    s = jnp.where(valid[None, :, None], s, NEG_INF)
    m = jnp.max(s, axis=-1, keepdims=True)
    p = jnp.exp(s - m)
    den = jnp.sum(p, axis=-1, keepdims=True)
    o = jnp.einsum('nbhqk,nbkhd->nbqhd', (p / den).astype(v.dtype), vb)
    lse = (m + jnp.log(den))[..., 0]
    return o.reshape(n, L, h, dh), jnp.swapaxes(lse, 2, 3).reshape(n, L, h)


def dilated_group(q, k, v, window, dilation):
    b, S, h, dh = q.shape
    L = S // dilation

    def strided(t):
        return t.reshape(b, L, dilation, h, dh).transpose(0, 2, 1, 3, 4).reshape(b * dilation, L, h, dh)

    o, lse = band_attention(strided(q), strided(k), strided(v), window // (2 * dilation))
    o = o.reshape(b, dilation, L, h, dh).transpose(0, 2, 1, 3, 4).reshape(b, S, h, dh)
    lse = lse.reshape(b, dilation, L, h).transpose(0, 2, 1, 3).reshape(b, S, h)
    return o, lse


def mixer_a(q, k, v):
    outs, lses = [], []
    for g, (window, dilation) in enumerate(A_PATTERNS):
        o, l = dilated_group(q[:, :, g], k[:, :, g], v[:, :, g], window, dilation)
        outs.append(o)
        lses.append(l)
    wts = jax.nn.softmax(jnp.stack(lses, axis=0), axis=0)
    o = jnp.sum(jnp.stack(outs, axis=0).astype(jnp.float32) * wts[..., None], axis=0)
    return o.astype(q.dtype)


def mixer_b(q, k, v, lam, lam_init, subln_g):
    b, S, h, _, dh = q.shape
    nb = S // QUERY_BLOCK
    qb = jnp.moveaxis(q.reshape(b, nb, QUERY_BLOCK, h, 2, dh), 1, 0)
    scale = dh ** -0.5

    def block(qblk):
        s = jnp.einsum('bqhmd,bkhmd->bhmqk', qblk, k).astype(jnp.float32) * scale
        p = jax.nn.softmax(s, axis=-1)
        a = p[:, :, 0] - lam * p[:, :, 1]
        return jnp.einsum('bhqk,bkhe->bqhe', a.astype(v.dtype), v)

    o = jnp.moveaxis(lax.map(block, qb), 0, 1).reshape(b, S, h, 2 * dh)
    return rms_norm(o, subln_g, SUBLN_EPS) * (1.0 - lam_init)


def mixer_c(q, k, v):
    b, S, n, g, dh = q.shape
    nb = S // QUERY_BLOCK
    qb = jnp.moveaxis(q.reshape(b, nb, QUERY_BLOCK, n, g, dh), 1, 0)
    scale = dh ** -0.5

    def block(qblk):
        s = jnp.einsum('bqngd,bsnd->bngqs', qblk, k).astype(jnp.float32) * scale
        p = jax.nn.softmax(s, axis=-1)
        return jnp.einsum('bngqs,bsnd->bqngd', p.astype(v.dtype), v)

    return jnp.moveaxis(lax.map(block, qb), 0, 1).reshape(b, S, n * g * dh)


def even_layer(h, cos, sin, w_in, qn_a, kn_a, qn_b, kn_b, lq1, lk1, lq2, lk2, subln_g, w_out, lam_init):
    b, S, _ = h.shape
    cuts = [A_QKV, 2 * A_QKV, 3 * A_QKV, 3 * A_QKV + B_QK, 3 * A_QKV + 2 * B_QK]
    aq, ak, av, bq, bk, bv = jnp.split(h @ w_in, cuts, axis=-1)
    shp_a = (b, S, A_GROUPS, A_HEADS_PER_GROUP, HEAD_DIM)
    aq = partial_rotary(rms_norm(aq.reshape(shp_a), qn_a), cos, sin)
    ak = partial_rotary(rms_norm(ak.reshape(shp_a), kn_a), cos, sin)
    oa = mixer_a(aq, ak, av.reshape(shp_a)).reshape(b, S, A_HEADS_PER_GROUP * HEAD_DIM)
    shp_b = (b, S, B_HEADS, 2, HEAD_DIM)
    bq = partial_rotary(rms_norm(bq.reshape(shp_b), qn_b), cos, sin)
    bk = partial_rotary(rms_norm(bk.reshape(shp_b), kn_b), cos, sin)
    bv = bv.reshape(b, S, B_HEADS, 2 * HEAD_DIM)
    f32 = jnp.float32
    lam = (jnp.exp(jnp.sum(lq1.astype(f32) * lk1.astype(f32)))
           - jnp.exp(jnp.sum(lq2.astype(f32) * lk2.astype(f32))) + lam_init)
    ob = mixer_b(bq, bk, bv, lam, lam_init, subln_g).reshape(b, S, B_V)
    return jnp.concatenate([oa, ob], axis=-1) @ w_out


def odd_layer(h, axial, w_in, qn, kn, w_out):
    b, S, _ = h.shape
    q, k, v = jnp.split(h @ w_in, [C_Q_HEADS * HEAD_DIM, (C_Q_HEADS + C_KV_HEADS) * HEAD_DIM], axis=-1)
    q = axial_rotary(rms_norm(q.reshape(b, S, C_KV_HEADS, C_GROUP, HEAD_DIM), qn), *axial)
    k = axial_rotary(rms_norm(k.reshape(b, S, C_KV_HEADS, HEAD_DIM), kn), *axial)
    v = v.reshape(b, S, C_KV_HEADS, HEAD_DIM)
    return mixer_c(q, k, v) @ w_out


def conv_ffn(h, w_up, conv_w, conv_b, w_down):
    u = h @ w_up
    up = jnp.pad(u, ((0, 0), (1, 1), (0, 0)))
    u = up[:, :-2] * conv_w[0] + up[:, 1:-1] * conv_w[1] + up[:, 2:] * conv_w[2] + conv_b
    gate, val = jnp.split(u, 2, axis=-1)
    return (jax.nn.silu(gate) * val) @ w_down


def trunk(x, norm_mix, norm_ffn, w_in_ab, q_norm_a, k_norm_a, q_norm_b, k_norm_b,
          lambda_q1, lambda_k1, lambda_q2, lambda_k2, subln_b, w_out_ab,
          w_in_c, q_norm_c, k_norm_c, w_out_c, w_up, conv_w, conv_b, w_down):
    S = x.shape[1]
    cos, sin = rope_cos_sin(jnp.arange(S), ROT_DIM, ROPE_THETA)
    rows = S // GRID_W
    row = jnp.repeat(jnp.arange(rows), GRID_W)
    col = jnp.tile(jnp.arange(GRID_W), rows)
    axial = rope_cos_sin(row, AXIAL_DIM, AXIAL_THETA) + rope_cos_sin(col, AXIAL_DIM, AXIAL_THETA)
    for i in range(DEPTH):
        h = rms_norm(x, norm_mix[i])
        j = i // 2
        if i % 2 == 0:
            lam_init = 0.8 - 0.6 * math.exp(-0.3 * i)
            x = x + even_layer(h, cos, sin, w_in_ab[j], q_norm_a[j], k_norm_a[j], q_norm_b[j], k_norm_b[j],
                               lambda_q1[j], lambda_k1[j], lambda_q2[j], lambda_k2[j], subln_b[j],
                               w_out_ab[j], lam_init)
        else:
            x = x + odd_layer(h, axial, w_in_c[j], q_norm_c[j], k_norm_c[j], w_out_c[j])
        x = x + conv_ffn(rms_norm(x, norm_ffn[i]), w_up[i], conv_w[i], conv_b[i], w_down[i])
    return x


def setup_inputs(seed: int = 0) -> dict:
    key = jax.random.key(seed)
    ks = jax.random.split(key, 32)
    f32 = jnp.float32

    def dense(k, shape, fan_in):
        return jax.random.normal(k, shape, f32) * (fan_in ** -0.5)

    def gain(k, shape):
        return 1.0 + 0.02 * jax.random.normal(k, shape, f32)

    return {
        'x_prompt': jax.random.normal(ks[0], (BATCH, SEQ, D_MODEL), f32),
        'x_sample': jax.random.normal(ks[1], (DEC_BATCH, DEC_SEQ, D_MODEL), f32),
        'norm_mix': gain(ks[2], (DEPTH, D_MODEL)),
        'norm_ffn': gain(ks[3], (DEPTH, D_MODEL)),
        'w_in_ab': dense(ks[4], (N_EVEN, D_MODEL, AB_IN), D_MODEL),
        'q_norm_a': gain(ks[5], (N_EVEN, HEAD_DIM)),
        'k_norm_a': gain(ks[6], (N_EVEN, HEAD_DIM)),
        'q_norm_b': gain(ks[7], (N_EVEN, HEAD_DIM)),
        'k_norm_b': gain(ks[8], (N_EVEN, HEAD_DIM)),
        'lambda_q1': 0.1 * jax.random.normal(ks[9], (N_EVEN, HEAD_DIM), f32),
        'lambda_k1': 0.1 * jax.random.normal(ks[10], (N_EVEN, HEAD_DIM), f32),
        'lambda_q2': 0.1 * jax.random.normal(ks[11], (N_EVEN, HEAD_DIM), f32),
        'lambda_k2': 0.1 * jax.random.normal(ks[12], (N_EVEN, HEAD_DIM), f32),
        'subln_b': gain(ks[13], (N_EVEN, 2 * HEAD_DIM)),
        'w_out_ab': dense(ks[14], (N_EVEN, AB_OUT, D_MODEL), AB_OUT),
        'w_in_c': dense(ks[15], (N_ODD, D_MODEL, C_IN), D_MODEL),
        'q_norm_c': gain(ks[16], (N_ODD, HEAD_DIM)),
        'k_norm_c': gain(ks[17], (N_ODD, HEAD_DIM)),
        'w_out_c': dense(ks[18], (N_ODD, C_OUT, D_MODEL), C_OUT),
        'w_up': dense(ks[19], (DEPTH, D_MODEL, 2 * D_FF), D_MODEL),
        'conv_w': dense(ks[20], (DEPTH, CONV_W, 2 * D_FF), CONV_W),
        'conv_b': 0.01 * jax.random.normal(ks[21], (DEPTH, 2 * D_FF), f32),
        'w_down': dense(ks[22], (DEPTH, D_FF, D_MODEL), D_FF),
    }


def reference(x_prompt, x_sample, norm_mix, norm_ffn, w_in_ab, q_norm_a, k_norm_a, q_norm_b, k_norm_b,
              lambda_q1, lambda_k1, lambda_q2, lambda_k2, subln_b, w_out_ab,
              w_in_c, q_norm_c, k_norm_c, w_out_c, w_up, conv_w, conv_b, w_down):
    y_prompt = trunk(x_prompt, norm_mix, norm_ffn, w_in_ab, q_norm_a, k_norm_a, q_norm_b, k_norm_b,
                     lambda_q1, lambda_k1, lambda_q2, lambda_k2, subln_b, w_out_ab,
                     w_in_c, q_norm_c, k_norm_c, w_out_c, w_up, conv_w, conv_b, w_down)
    y_sample = trunk(x_sample, norm_mix, norm_ffn, w_in_ab, q_norm_a, k_norm_a, q_norm_b, k_norm_b,
                     lambda_q1, lambda_k1, lambda_q2, lambda_k2, subln_b, w_out_ab,
                     w_in_c, q_norm_c, k_norm_c, w_out_c, w_up, conv_w, conv_b, w_down)
    return (y_prompt, y_sample)
```

```python
import math
from contextlib import ExitStack

import numpy as np
import ml_dtypes

import concourse.bass as bass
import concourse.mybir as mybir
from concourse.bass_utils import run_bass_kernel_spmd

F32 = mybir.dt.float32
BF16 = mybir.dt.bfloat16
AF = mybir.ActivationFunctionType
ALU = mybir.AluOpType

D = 1024
DFF = 2816
NPAIR = DFF // 128
A_DIL = (1, 4, 16)
MASK_REL = {0: (-1, 4), 1: (-2, 5), 2: (-8, 11)}
MASK_BASE = {0: 0, 1: 6, 2: 14}
NMASK = 34
SCALE = 0.125
TABLE_S = 8192


class Buf:
    __slots__ = ("w", "r")

    def __init__(self):
        self.w = {}
        self.r = {}


class TT:
    def __init__(self, t):
        self.t = t
        self.b = Buf()

    def __getitem__(self, idx):
        return self.t[idx]


class Eng:
    def __init__(self, name, q, sem):
        self.name = name
        self.q = q
        self.sem = sem
        self.count = 0
        self.seen = {}


class DSlot:
    def __init__(self, sem, key):
        self.sem = sem
        self.key = key
        self.issued = 0


def _merge(d, s):
    for k, v in s.items():
        if d.get(k, 0) < v:
            d[k] = v


class Ctx:
    def __init__(self, nc, stack, nslots=8):
        self.nc = nc
        self.stack = stack
        self.E = {}
        self.sems = {}
        for name, q in (("pe", nc.tensor), ("act", nc.scalar), ("dve", nc.vector),
                        ("pool", nc.gpsimd), ("sp", nc.sync)):
            sem = stack.enter_context(nc.semaphore("sem_" + name))
            self.E[name] = Eng(name, q, sem)
            self.sems[name] = sem
        self.slots = {}
        self.nexts = {}
        for qn, n in (("sp", nslots), ("pool", 4)):
            lst = []
            for i in range(n):
                key = "d_%s%d" % (qn, i)
                sem = stack.enter_context(nc.semaphore(key))
                self.sems[key] = sem
                lst.append(DSlot(sem, key))
            self.slots[qn] = lst
            self.nexts[qn] = 0

    def _wait(self, E, deps):
        for key, val in deps.items():
            if key == E.name:
                continue
            if E.seen.get(key, 0) >= val:
                continue
            E.q.wait_ge(self.sems[key], val)
            E.seen[key] = val

    def op(self, en, fn, reads=(), writes=()):
        E = self.E[en]
        deps = {}
        for b in reads:
            _merge(deps, b.w)
        for b in writes:
            _merge(deps, b.w)
            _merge(deps, b.r)
        self._wait(E, deps)
        ins = fn(E.q)
        E.count += 1
        ins.then_inc(E.sem, 1)
        c = E.count
        for b in reads:
            b.r[en] = c
        for b in writes:
            b.w[en] = c
        return ins

    def dma(self, en, xfers, reads=(), writes=(), **kw):
        E = self.E[en]
        deps = {}
        for b in reads:
            _merge(deps, b.w)
        for b in writes:
            _merge(deps, b.w)
            _merge(deps, b.r)
        self._wait(E, deps)
        lst = self.slots[en]
        slot = lst[self.nexts[en]]
        self.nexts[en] = (self.nexts[en] + 1) % len(lst)
        if E.seen.get(slot.key, 0) < slot.issued:
            E.q.wait_ge(slot.sem, slot.issued)
            E.seen[slot.key] = slot.issued
        for (o, i) in xfers:
            E.q.dma_start(out=o, in_=i, **kw).then_inc(slot.sem, 16)
            slot.issued += 16
        for b in reads:
            b.r[slot.key] = slot.issued
        for b in writes:
            b.w[slot.key] = slot.issued

    def barrier(self):
        deps = {}
        for name, E in self.E.items():
            if E.count:
                deps[name] = E.count
        for lst in self.slots.values():
            for s in lst:
                if s.issued:
                    deps[s.key] = s.issued
        for name in ("pe", "act", "dve", "sp"):
            self._wait(self.E[name], deps)


def _rope(pos, dim, theta):
    inv = (np.float32(theta) ** (-np.arange(0, dim, 2, dtype=np.float32) / np.float32(dim))).astype(np.float32)
    ang = pos.astype(np.float32)[:, None] * inv[None, :]
    return np.cos(ang).astype(np.float32), np.sin(ang).astype(np.float32)


def host_constants():
    S = TABLE_S
    tabs = np.zeros((2, 2, 128, S), np.float32)
    perms = np.zeros((2, 128, 128), np.float32)
    pos = np.arange(S)
    cos, sin = _rope(pos, 16, 500000.0)
    cr, sr = _rope(pos // 64, 32, 10000.0)
    cc, sc = _rope(pos % 64, 32, 10000.0)
    for p in range(128):
        d = p % 64
        hb = p - d
        if d < 8:
            tabs[0, 0, p] = cos[:, d]
            tabs[0, 1, p] = -sin[:, d]
            perms[0, hb + d + 8, p] = 1.0
        elif d < 16:
            tabs[0, 0, p] = cos[:, d - 8]
            tabs[0, 1, p] = sin[:, d - 8]
            perms[0, hb + d - 8, p] = 1.0
        else:
            tabs[0, 0, p] = 1.0
        if d < 16:
            tabs[1, 0, p] = cr[:, d]
            tabs[1, 1, p] = -sr[:, d]
            perms[1, hb + d + 16, p] = 1.0
        elif d < 32:
            tabs[1, 0, p] = cr[:, d - 16]
            tabs[1, 1, p] = sr[:, d - 16]
            perms[1, hb + d - 16, p] = 1.0
        elif d < 48:
            tabs[1, 0, p] = cc[:, d - 32]
            tabs[1, 1, p] = -sc[:, d - 32]
            perms[1, hb + d + 16, p] = 1.0
        else:
            tabs[1, 0, p] = cc[:, d - 48]
            tabs[1, 1, p] = sc[:, d - 48]
            perms[1, hb + d - 16, p] = 1.0
    cm = np.zeros((128, 5, 128), np.float32)
    cm[:, 0] = np.eye(128, dtype=np.float32)
    cm[:, 1] = 1.0
    cm[0:64, 2, 0:64] = 1.0
    cm[64:128, 2, 64:128] = 1.0
    cm[:, 3] = perms[0]
    cm[:, 4] = perms[1]
    masks = np.zeros((128, NMASK, 512), np.float32)
    i = np.arange(128)[:, None]
    j = np.arange(512)[None, :]
    for g, d in enumerate(A_DIL):
        lo, hi = MASK_REL[g]
        for rel in range(lo, hi + 1):
            diff = rel * 128 + i - j
            ok = (np.abs(diff) <= 64 * d) & (diff % d == 0)
            masks[:, MASK_BASE[g] + rel - lo, :] = (ok.astype(np.float32) - 1.0) * 30000.0
    return tabs, cm.astype(ml_dtypes.bfloat16), masks.astype(ml_dtypes.bfloat16)


def build_program(seq_lens, depth=4, debug=False):
    TOK = sum(seq_lens)
    seqs = []
    o = 0
    for S in seq_lens:
        seqs.append((o, S))
        o += S
    n_even = (depth + 1) // 2
    n_odd = depth // 2

    nc = bass.Bass("TRN2", target_bir_lowering=False)

    def din(name, shape, dt=F32):
        return nc.dram_tensor(name, list(shape), dt, kind="ExternalInput").ap()

    def dscr(name, shape, dt):
        return nc.dram_tensor(name, list(shape), dt, kind="Internal").ap()

    x_all = din("x_all", [TOK, D])
    y_all = nc.dram_tensor("y_all", [TOK, D], F32, kind="ExternalOutput").ap()
    norm_mix = din("norm_mix", [depth, D])
    norm_ffn = din("norm_ffn", [depth, D])
    w_in_ab = din("w_in_ab", [n_even, D, 3840])
    qn_a = din("q_norm_a", [n_even, 64])
    kn_a = din("k_norm_a", [n_even, 64])
    qn_b = din("q_norm_b", [n_even, 64])
    kn_b = din("k_norm_b", [n_even, 64])
    lq1 = din("lambda_q1", [n_even, 64])
    lk1 = din("lambda_k1", [n_even, 64])
    lq2 = din("lambda_q2", [n_even, 64])
    lk2 = din("lambda_k2", [n_even, 64])
    subln = din("subln_b", [n_even, 128])
    w_out_ab = din("w_out_ab", [n_even, 768, D])
    w_in_c = din("w_in_c", [max(n_odd, 1), D, 1536])
    qn_c = din("q_norm_c", [max(n_odd, 1), 64])
    kn_c = din("k_norm_c", [max(n_odd, 1), 64])
    w_out_c = din("w_out_c", [max(n_odd, 1), D, D])
    w_up = din("w_up", [depth, D, 2 * DFF])
    conv_w = din("conv_w", [depth, 3, 2 * DFF])
    conv_b = din("conv_b", [depth, 2 * DFF])
    w_down = din("w_down", [depth, DFF, D])
    tabs_in = din("tabs", [2, 2, 128, TABLE_S])
    cm_in = din("cmats", [128, 5, 128], BF16)
    ident_in = din("ident", [128, 128])
    masks_in = din("masks", [128, NMASK, 512], BF16)

    xT = [dscr("xT0", [D, TOK], F32), dscr("xT1", [D, TOK], F32)]
    qk_d = dscr("qk_d", [2560, TOK], BF16)
    v_d = dscr("v_d", [TOK, 1280], BF16)
    at_d = dscr("at_d", [D, TOK], BF16)
    win_ab_d = dscr("win_ab_d", [n_even, D, 3840], BF16)
    wout_ab_d = dscr("wout_ab_d", [n_even, 768, D], BF16)
    win_c_d = dscr("win_c_d", [max(n_odd, 1), D, 1536], BF16)
    wout_c_d = dscr("wout_c_d", [max(n_odd, 1), D, D], BF16)
    wdown_d = dscr("wdown_d", [depth, DFF, D], BF16)
    wup_nat = dscr("wup_nat", [depth, D, 2 * DFF], BF16)
    wup_p = dscr("wup_p", [depth, NPAIR, 128, 8, 2, 128], BF16)

    with ExitStack() as top:
        cx = Ctx(nc, top)
        op, dma = cx.op, cx.dma

        uid = [0]

        def sb(stack, name, shape, dt):
            uid[0] += 1
            return TT(stack.enter_context(nc.sbuf_tensor("%s_u%d" % (name, uid[0]), list(shape), dt)))

        ps = top.enter_context(nc.psum_tensor("ps", [128, 8, 512], F32))
        bank_b = [Buf() for _ in range(8)]

        cmats = sb(top, "cmats", [128, 5, 128], BF16)
        ident_f = sb(top, "ident_f", [128, 128], F32)
        cst = sb(top, "cst", [128, 4], F32)
        dma("sp", [(cmats[:], cm_in[:, :, :])], writes=[cmats.b])
        dma("sp", [(ident_f[:], ident_in[:, :])], writes=[ident_f.b])
        op("dve", lambda q: q.memset(cst[:, 0:1], 1e-6), writes=[cst.b])
        op("dve", lambda q: q.memset(cst[:, 1:2], 1e-5), writes=[cst.b])
        ones_m = cmats[:, 1, :]
        blk_m = cmats[:, 2, :]

        xT_b = [Buf(), Buf()]
        qk_b, v_b, at_b = Buf(), Buf(), Buf()
        wts_b = Buf()

        cstack = ExitStack()
        cin = [sb(cstack, "cast_i%d" % i, [128, 4096], F32) for i in range(2)]
        cout = [sb(cstack, "cast_o%d" % i, [128, 4096], BF16) for i in range(2)]
        cctr = [0]

        def cast_flat(dst, src, n_elems):
            x = n_elems // 128
            assert x * 128 == n_elems
            s2 = src.rearrange("(p x) -> p x", p=128)
            d2 = dst.rearrange("(p x) -> p x", p=128)
            c0 = 0
            while c0 < x:
                c1 = min(x, c0 + 4096)
                w = c1 - c0
                i = cctr[0] % 2
                cctr[0] += 1
                ci, co = cin[i], cout[i]
                dma("sp", [(ci[:, 0:w], s2[:, c0:c1])], writes=[ci.b])
                if i == 0:
                    op("dve", lambda q: q.tensor_copy(out=co[:, 0:w], in_=ci[:, 0:w]), reads=[ci.b], writes=[co.b])
                else:
                    op("act", lambda q: q.activation(out=co[:, 0:w], in_=ci[:, 0:w], func=AF.Copy), reads=[ci.b], writes=[co.b])
                dma("sp", [(d2[:, c0:c1], co[:, 0:w])], reads=[co.b], writes=[wts_b])
                c0 = c1

        def flat2(ap2):
            return ap2.rearrange("b c -> (b c)")

        deferred = []
        deferred_rearr = []

        def cast_later(dst, src, n_elems):
            x = n_elems // 128
            s2 = src.rearrange("(p x) -> p x", p=128)
            d2 = dst.rearrange("(p x) -> p x", p=128)
            c0 = 0
            while c0 < x:
                c1 = min(x, c0 + 2048)
                deferred.append((d2[:, c0:c1], s2[:, c0:c1], c1 - c0))
                c0 = c1

        def rearr_wup(L):
            for m in range(NPAIR):
                xf = []
                for t in range(2):
                    src = wup_nat[L, :, t * DFF + m * 128: t * DFF + (m + 1) * 128].rearrange("(k p) c -> p k c", p=128)
                    xf.append((wup_p[L, m, :, :, t, :], src))
                dma("sp", xf, reads=[wts_b], writes=[wupp_b])

        wupp_b = Buf()
        for L in range(depth):
            fn = cast_flat if L == 0 else cast_later
            jj_ = L // 2
            if L % 2 == 0:
                fn(flat2(win_ab_d[jj_]), flat2(w_in_ab[jj_]), D * 3840)
                fn(flat2(wout_ab_d[jj_]), flat2(w_out_ab[jj_]), 768 * D)
            else:
                fn(flat2(win_c_d[jj_]), flat2(w_in_c[jj_]), D * 1536)
                fn(flat2(wout_c_d[jj_]), flat2(w_out_c[jj_]), D * D)
            fn(flat2(wdown_d[L]), flat2(w_down[L]), DFF * D)
            fn(flat2(wup_nat[L]), flat2(w_up[L]), D * 2 * DFF)
            if L == 0:
                rearr_wup(0)
            else:
                deferred_rearr.append(L)
        cx.barrier()
        cstack.close()

        def phase_transpose_in():
            with ExitStack() as st:
                xin = [sb(st, "p0_x%d" % i, [128, 4, D], F32) for i in range(2)]
                xo = [sb(st, "p0_o%d" % i, [128, 8, 512], F32) for i in range(2)]
                ntile = TOK // 512
                xv = xT[0].rearrange("(c p) t -> p c t", p=128)

                def load(i):
                    T0 = i * 512
                    dma("sp", [(xin[i % 2][:], x_all[T0:T0 + 512, :].rearrange("(j p) f -> p j f", p=128))],
                        writes=[xin[i % 2].b])

                load(0)
                for i in range(ntile):
                    if i + 1 < ntile:
                        load(i + 1)
                    xi, xot = xin[i % 2], xo[i % 2]
                    for c in range(8):
                        bk = c % 8
                        for j in range(4):
                            op("pe", lambda q: q.transpose(ps[:, bk, j * 128:(j + 1) * 128],
                                                           xi[:, j, c * 128:(c + 1) * 128], ident_f[:]),
                               reads=[xi.b, ident_f.b], writes=[bank_b[bk]])
                        if c % 2 == 0:
                            op("act", lambda q: q.activation(out=xot[:, c, :], in_=ps[:, bk, :], func=AF.Copy),
                               reads=[bank_b[bk]], writes=[xot.b])
                        else:
                            op("dve", lambda q: q.tensor_copy(out=xot[:, c, :], in_=ps[:, bk, :]),
                               reads=[bank_b[bk]], writes=[xot.b])
                    T0 = i * 512
                    dma("sp", [(xv[:, :, T0:T0 + 512], xot[:])], reads=[xot.b], writes=[xT_b[0]])
                cx.barrier()

        def phase_transpose_out(cur):
            with ExitStack() as st:
                xin = [sb(st, "pf_x%d" % i, [128, 8, 512], F32) for i in range(2)]
                yo = [sb(st, "pf_o%d" % i, [128, 4, D], F32) for i in range(2)]
                ntile = TOK // 512
                xv = xT[cur].rearrange("(c p) t -> p c t", p=128)

                def load(i):
                    T0 = i * 512
                    dma("sp", [(xin[i % 2][:], xv[:, :, T0:T0 + 512])], reads=[xT_b[cur]], writes=[xin[i % 2].b])

                load(0)
                for i in range(ntile):
                    if i + 1 < ntile:
                        load(i + 1)
                    xi, yt = xin[i % 2], yo[i % 2]
                    for j in range(4):
                        for hh in range(2):
                            bk = (j * 2 + hh) % 8
                            for cc in range(4):
                                c = hh * 4 + cc
                                op("pe", lambda q: q.transpose(ps[:, bk, cc * 128:(cc + 1) * 128],
                                                               xi[:, c, j * 128:(j + 1) * 128], ident_f[:]),
                                   reads=[xi.b, ident_f.b], writes=[bank_b[bk]])
                            if hh == 0:
                                op("act", lambda q: q.activation(out=yt[:, j, hh * 512:(hh + 1) * 512],
                                                                 in_=ps[:, bk, :], func=AF.Copy),
                                   reads=[bank_b[bk]], writes=[yt.b])
                            else:
                                op("dve", lambda q: q.tensor_copy(out=yt[:, j, hh * 512:(hh + 1) * 512],
                                                                  in_=ps[:, bk, :]),
                                   reads=[bank_b[bk]], writes=[yt.b])
                    T0 = i * 512
                    dma("sp", [(y_all[T0:T0 + 512, :].rearrange("(j p) f -> p j f", p=128), yt[:])],
                        reads=[yt.b], writes=[Buf()])
                cx.barrier()

        def load_vec64(dst, col, src_row):
            s = src_row.rearrange("(p o) -> p o", o=1)
            dma("sp", [(dst[0:64, col:col + 1], s), (dst[64:128, col:col + 1], s)], writes=[dst.b])

        def rms_stats(src, n, sq, bank, lnv, rstd, eps_col, inv_n, nch=8):
            op("act", lambda q: q.activation(out=sq[:, 0:nch, 0:n], in_=src[:, 0:nch, 0:n], func=AF.Square),
               reads=[src.b], writes=[sq.b])
            for c in range(nch):
                op("pe", lambda q: q.matmul(ps[:, bank, 0:n], ones_m, sq[:, c, 0:n], start=(c == 0), stop=(c == nch - 1)),
                   reads=[sq.b, cmats.b], writes=[bank_b[bank]])
            op("act", lambda q: q.activation(out=lnv[:, 0:n], in_=ps[:, bank, 0:n], func=AF.Ln,
                                             bias=cst[:, eps_col:eps_col + 1], scale=inv_n),
               reads=[bank_b[bank], cst.b], writes=[lnv.b])
            op("act", lambda q: q.activation(out=rstd[:, 0:n], in_=lnv[:, 0:n], func=AF.Exp, scale=-0.5),
               reads=[lnv.b], writes=[rstd.b])

        def phase_in_proj(L, cur):
            even = (L % 2 == 0)
            jj = L // 2
            with ExitStack() as st, nc.allow_non_contiguous_dma(reason="small parameter vectors"):
                if even:
                    Wd, NF = win_ab_d[jj], 3840
                    fm_cols = [128 * i for i in range(6)] + [768 + 128 * i for i in range(6)] + \
                              [2304 + 128 * i for i in range(4)] + [2816 + 128 * i for i in range(4)]
                    fm_gain = [0] * 6 + [1] * 6 + [2] * 4 + [3] * 4
                    vpieces = [(1536, 512, 0), (2048, 256, 512), (3328, 512, 768)]
                    nv = 1280
                    lt = 0
                else:
                    Wd, NF = win_c_d[jj], 1536
                    fm_cols = [128 * i for i in range(10)]
                    fm_gain = [0] * 8 + [1] * 2
                    vpieces = [(1280, 256, 0)]
                    nv = 256
                    lt = 1
                nfm = len(fm_cols)
                permM = cmats[:, 3 + lt, :]
                Wsb = sb(st, "p1_w", [128, 8, NF], BF16)
                dma("sp", [(Wsb[:], Wd.rearrange("(k p) n -> p k n", p=128))], reads=[wts_b], writes=[Wsb.b])
                gv = sb(st, "p1_g", [128, 4], F32)
                if even:
                    for col, src in enumerate((qn_a, kn_a, qn_b, kn_b)):
                        load_vec64(gv, col, src[jj])
                else:
                    for col, src in enumerate((qn_c, kn_c)):
                        load_vec64(gv, col, src[jj])
                gmix = sb(st, "p1_gm", [128, 8], F32)
                dma("sp", [(gmix[:], norm_mix[L].rearrange("(c p) -> p c", p=128))], writes=[gmix.b])
                xt = [sb(st, "p1_x%d" % i, [128, 8, 512], F32) for i in range(3)]
                cs = [sb(st, "p1_cs%d" % i, [128, 2, 512], F32) for i in range(3)]
                sq = sb(st, "p1_sq", [128, 8, 512], BF16)
                hT = [sb(st, "p1_h%d" % i, [128, 8, 512], BF16) for i in range(2)]
                lnv = sb(st, "p1_ln", [128, 512], F32)
                rstd = sb(st, "p1_rs", [128, 512], F32)
                sqc = [sb(st, "p1_sqc%d" % i, [128, 512], BF16) for i in range(2)]
                lnq = [sb(st, "p1_lnq%d" % i, [128, 512], F32) for i in range(2)]
                rsq = [sb(st, "p1_rsq%d" % i, [128, 512], F32) for i in range(2)]
                av = [sb(st, "p1_a%d" % i, [128, 512], BF16) for i in range(2)]
                t1 = [sb(st, "p1_t1%d" % i, [128, 512], F32) for i in range(2)]
                t2 = [sb(st, "p1_t2%d" % i, [128, 512], F32) for i in range(2)]
                qkr = [sb(st, "p1_qk%d" % i, [128, 512], BF16) for i in range(4)]
                vr = [sb(st, "p1_v%d" % i, [128, nv], BF16) for i in range(2)]
                xv = xT[cur].rearrange("(c p) t -> p c t", p=128)
                qkv = qk_d.rearrange("(c p) t -> p c t", p=128)

                tiles = []
                for (s0, S) in seqs:
                    for t0 in range(0, S, 512):
                        tiles.append((s0 + t0, t0))
                nt = len(tiles)

                def load(i):
                    T0, t0 = tiles[i]
                    dma("sp", [(xt[i % 3][:], xv[:, :, T0:T0 + 512])], reads=[xT_b[cur]], writes=[xt[i % 3].b])
                    dma("sp", [(cs[i % 3][:], tabs_in[lt, :, :, t0:t0 + 512].rearrange("a p t -> p a t"))],
                        writes=[cs[i % 3].b])

                def norm_stage(i):
                    x_, h_ = xt[i % 3], hT[i % 2]
                    rms_stats(x_, 512, sq, 7, lnv, rstd, 0, 1.0 / D)
                    for c in range(8):
                        op("dve", lambda q: q.scalar_tensor_tensor(out=h_[:, c, :], in0=x_[:, c, :],
                                                                   scalar=gmix[:, c:c + 1], in1=rstd[:],
                                                                   op0=ALU.mult, op1=ALU.mult),
                           reads=[x_.b, gmix.b, rstd.b], writes=[h_.b])

                vjobs = [(s_, p) for s_ in range(4) for p in vpieces]

                def S1(i, c, j):
                    h_ = hT[i % 2]
                    bk = j % 3
                    wc = fm_cols[c]
                    for k in range(8):
                        op("pe", lambda q: q.matmul(ps[:, bk, :], Wsb[:, k, wc:wc + 128], h_[:, k, :],
                                                    start=(k == 0), stop=(k == 7)),
                           reads=[Wsb.b, h_.b], writes=[bank_b[bk]])
                    op("act", lambda q: q.activation(out=sqc[j % 2][:], in_=ps[:, bk, :], func=AF.Square),
                       reads=[bank_b[bk]], writes=[sqc[j % 2].b])

                def S2(i, c, j):
                    bk, b2 = j % 3, 3 + j % 2
                    op("pe", lambda q: q.matmul(ps[:, b2, :], blk_m, sqc[j % 2][:], start=True, stop=True),
                       reads=[sqc[j % 2].b, cmats.b], writes=[bank_b[b2]])
                    op("act", lambda q: q.activation(out=lnq[j % 2][:], in_=ps[:, b2, :], func=AF.Ln,
                                                     bias=cst[:, 0:1], scale=1.0 / 64),
                       reads=[bank_b[b2], cst.b], writes=[lnq[j % 2].b])
                    op("act", lambda q: q.activation(out=rsq[j % 2][:], in_=lnq[j % 2][:], func=AF.Exp, scale=-0.5),
                       reads=[lnq[j % 2].b], writes=[rsq[j % 2].b])
                    gi = fm_gain[c]
                    op("dve", lambda q: q.scalar_tensor_tensor(out=av[j % 2][:], in0=ps[:, bk, :],
                                                               scalar=gv[:, gi:gi + 1], in1=rsq[j % 2][:],
                                                               op0=ALU.mult, op1=ALU.mult),
                       reads=[bank_b[bk], gv.b, rsq[j % 2].b], writes=[av[j % 2].b])

                def S3(i, c, j):
                    T0, t0 = tiles[i]
                    cs_ = cs[i % 3]
                    b3 = 5 + j % 2
                    op("pe", lambda q: q.matmul(ps[:, b3, :], permM, av[j % 2][:], start=True, stop=True),
                       reads=[av[j % 2].b, cmats.b], writes=[bank_b[b3]])
                    op("dve", lambda q: q.tensor_tensor(out=t1[j % 2][:], in0=av[j % 2][:], in1=cs_[:, 0, :], op=ALU.mult),
                       reads=[av[j % 2].b, cs_.b], writes=[t1[j % 2].b])
                    op("dve", lambda q: q.tensor_tensor(out=t2[j % 2][:], in0=ps[:, b3, :], in1=cs_[:, 1, :], op=ALU.mult),
                       reads=[bank_b[b3], cs_.b], writes=[t2[j % 2].b])
                    qo = qkr[j % 4]
                    op("dve", lambda q: q.tensor_tensor(out=qo[:], in0=t1[j % 2][:], in1=t2[j % 2][:], op=ALU.add),
                       reads=[t1[j % 2].b, t2[j % 2].b], writes=[qo.b])
                    dma("sp", [(qk_d[c * 128:(c + 1) * 128, T0:T0 + 512], qo[:])], reads=[qo.b], writes=[qk_b])

                def VJ(i, idx):
                    T0, t0 = tiles[i]
                    h_ = hT[i % 2]
                    s_, (wc, n, dc) = vjobs[idx]
                    for k in range(8):
                        op("pe", lambda q: q.matmul(ps[:, 7, 0:n], h_[:, k, s_ * 128:(s_ + 1) * 128], Wsb[:, k, wc:wc + n],
                                                    start=(k == 0), stop=(k == 7)),
                           reads=[Wsb.b, h_.b], writes=[bank_b[7]])
                    vo = vr[s_ % 2]
                    op("act", lambda q: q.activation(out=vo[:, dc:dc + n], in_=ps[:, 7, 0:n], func=AF.Copy),
                       reads=[bank_b[7]], writes=[vo.b])
                    if idx % len(vpieces) == len(vpieces) - 1:
                        dma("sp", [(v_d[T0 + s_ * 128:T0 + (s_ + 1) * 128, 0:nv], vo[:])], reads=[vo.b], writes=[v_b])

                jobs = [(i, c) for i in range(nt) for c in range(nfm)]
                nj = len(jobs)
                assert len(vjobs) <= nfm
                load(0)
                if nt > 1:
                    load(1)
                norm_stage(0)
                for s_i in range(nj + 2):
                    if s_i < nj:
                        i, c = jobs[s_i]
                        if c == 2 and i + 2 < nt:
                            load(i + 2)
                        if c == nfm // 2 and i + 1 < nt:
                            norm_stage(i + 1)
                        S1(i, c, s_i)
                    if 0 <= s_i - 1 < nj:
                        S2(jobs[s_i - 1][0], jobs[s_i - 1][1], s_i - 1)
                    if 0 <= s_i - 2 < nj:
                        S3(jobs[s_i - 2][0], jobs[s_i - 2][1], s_i - 2)
                    if s_i < nj and c < len(vjobs):
                        VJ(i, c)
                cx.barrier()

        class Unit:
            __slots__ = ("pre", "qk", "nb", "mask", "pv", "fin", "rd", "den")

            def __init__(self):
                self.den = None

        def run_units(units, P, Psm, masks_t, bg=None, bg_every=8, lazy=None):
            n = len(units)
            ident_b = cmats[:, 0, :]
            pending = [None]

            def qk_exp(u):
                un = units[u]
                g0 = (u % 2) * 2
                for e, (lhsT, rhs) in enumerate(un.qk):
                    if un.mask is None:
                        op("pe", lambda q: q.matmul(ps[:, g0 + e, :], lhsT, rhs, start=True, stop=True),
                           reads=un.rd, writes=[bank_b[g0 + e]])
                    else:
                        mi = un.mask + e
                        op("pe", lambda q: q.matmul(ps[:, g0 + e, :], lhsT, rhs, start=True, stop=False),
                           reads=un.rd, writes=[bank_b[g0 + e]])
                        op("pe", lambda q: q.matmul(ps[:, g0 + e, :], ident_b, masks_t[:, mi, :], start=False, stop=True),
                           reads=[cmats.b, masks_t.b], writes=[bank_b[g0 + e]])
                nb = un.nb
                Pt = P[u % 3]
                op("act", lambda q: q.activation(out=Pt[:, 0:nb, :], in_=ps[:, g0:g0 + nb, :], func=AF.Exp, scale=SCALE),
                   reads=[bank_b[g0 + e_] for e_ in range(nb)], writes=[Pt.b])
                if un.den is not None:
                    Pq = Psm[u % 3]
                    op("dve", lambda q: q.tensor_tensor(out=Pq[:], in0=Pt[:, 0, :], in1=Pt[:, 1, :], op=ALU.add),
                       reads=[Pt.b], writes=[Pq.b])
                if lazy:
                    lazy.pop(0)()

            def den_mm(u):
                un = units[u]
                bank, start, stop = un.den
                Pq = Psm[u % 3]
                op("pe", lambda q: q.matmul(ps[:, bank, :], ones_m, Pq[:], start=start, stop=stop),
                   reads=[Pq.b, cmats.b], writes=[bank_b[bank]])

            def pv(u):
                un = units[u]
                Pt = P[u % 3]
                if pending[0] is not None:
                    den_mm(pending[0])
                    pending[0] = None
                for (bank, lhsT, e, start, stop, rd) in un.pv:
                    op("pe", lambda q: q.matmul(ps[:, bank, :], lhsT, Pt[:, e, :], start=start, stop=stop),
                       reads=[Pt.b] + rd, writes=[bank_b[bank]])
                if un.den is not None:
                    if un.fin is not None:
                        den_mm(u)
                    else:
                        pending[0] = u
                if un.fin is not None:
                    un.fin()

            for u in range(n + 1):
                if u < n:
                    qk_exp(u)
                if u >= 1:
                    pv(u - 1)
                if u < n and units[u].pre is not None:
                    units[u].pre()
                if bg and u % bg_every == bg_every - 1:
                    bg.pop(0)()
            while bg:
                bg.pop(0)()
            while lazy:
                lazy.pop(0)()

        def fin_pair(accA, accB, rec, ot, dst_ap, dst_b):
            op("dve", lambda q: q.reciprocal(out=rec[0:64, :], in_=ps[64:128, accA, :]),
               reads=[bank_b[accA]], writes=[rec.b])
            op("dve", lambda q: q.tensor_tensor(out=ot[0:64, :], in0=ps[0:64, accA, :], in1=rec[0:64, :], op=ALU.mult),
               reads=[bank_b[accA], rec.b], writes=[ot.b])
            op("dve", lambda q: q.reciprocal(out=rec[64:128, :], in_=ps[0:64, accB, :]),
               reads=[bank_b[accB]], writes=[rec.b])
            op("dve", lambda q: q.tensor_tensor(out=ot[64:128, :], in0=ps[64:128, accB, :], in1=rec[64:128, :], op=ALU.mult),
               reads=[bank_b[accB], rec.b], writes=[ot.b])
            dma("sp", [(dst_ap, ot[:])], reads=[ot.b], writes=[dst_b])

        def phase_attn_c(L):
            with ExitStack() as st:
                KT = [[sb(st, "c_k%d_%d" % (i, v), [128, TABLE_S], BF16) for v in range(2)] for i in range(2)]
                for kk in KT:
                    for k_ in kk:
                        op("dve", lambda q: q.memset(k_[:], 0.0), writes=[k_.b])
                VE = [sb(st, "c_v%d" % i, [128, TABLE_S // 128, 192], BF16) for i in range(2)]
                QT = [sb(st, "c_q%d" % i, [128, TABLE_S], BF16) for i in range(2)]
                P = [sb(st, "c_p%d" % i, [128, 2, 512], BF16) for i in range(3)]
                rec = [sb(st, "c_r%d" % i, [128, 512], F32) for i in range(2)]
                ot = [sb(st, "c_o%d" % i, [128, 512], BF16) for i in range(2)]
                for v in VE:
                    op("dve", lambda q: q.memset(v[:], 1.0), writes=[v.b])
                jobs = []
                for (s0, S) in seqs:
                    for n in range(4):
                        for i in range(2):
                            jobs.append((s0, S, n, i))

                def load_kv(jn):
                    s0, S, n, i = jobs[jn]
                    kv = jn // 2
                    src = qk_d[1024 + n * 64:1024 + (n + 1) * 64, s0:s0 + S]
                    dma("sp", [(KT[kv % 2][0][0:64, 0:S], src), (KT[kv % 2][1][64:128, 0:S], src)],
                        reads=[qk_b], writes=[KT[kv % 2][0].b, KT[kv % 2][1].b])
                    dma("sp", [(VE[kv % 2][:, 0:S // 128, 64:128],
                                v_d[s0:s0 + S, n * 64:(n + 1) * 64].rearrange("(c p) f -> p c f", p=128))],
                        reads=[v_b], writes=[VE[kv % 2].b])

                def load_q(jn):
                    s0, S, n, i = jobs[jn]
                    qc = 2 * n + i
                    dma("sp", [(QT[jn % 2][:, 0:S], qk_d[qc * 128:(qc + 1) * 128, s0:s0 + S])],
                        reads=[qk_b], writes=[QT[jn % 2].b])

                units = []
                fcount = [0]
                for jn, (s0, S, n, i) in enumerate(jobs):
                    kv = jn // 2
                    K_, V_, Q_ = KT[kv % 2], VE[kv % 2], QT[jn % 2]
                    qc = 2 * n + i
                    nkp = S // 256
                    for qt in range(S // 512):
                        par = fcount[0] % 2
                        fcount[0] += 1
                        acc = (4 + 2 * par, 5 + 2 * par)
                        for kp in range(nkp):
                            for hb in range(2):
                                un = Unit()
                                un.pre = None
                                if qt == 0 and kp == 0 and hb == 0:
                                    def pre(jn=jn):
                                        if jn + 1 < len(jobs):
                                            if (jn + 1) % 2 == 0:
                                                load_kv(jn + 1)
                                            load_q(jn + 1)
                                    un.pre = pre
                                un.qk = [(K_[hb][:, (2 * kp + e) * 128:(2 * kp + e + 1) * 128],
                                          Q_[:, qt * 512:(qt + 1) * 512]) for e in range(2)]
                                un.nb = 2
                                un.mask = None
                                un.rd = [K_[hb].b, Q_.b]
                                vs = slice(64, 192) if hb == 0 else slice(0, 128)
                                un.pv = [(acc[hb], V_[:, 2 * kp + e, vs], e, (kp == 0 and e == 0),
                                          (kp == nkp - 1 and e == 1), [V_.b]) for e in range(2)]
                                un.fin = None
                                if kp == nkp - 1 and hb == 1:
                                    def fin(acc=acc, par=par, qc=qc, s0=s0, qt=qt):
                                        fin_pair(acc[0], acc[1], rec[par], ot[par],
                                                 at_d[qc * 128:(qc + 1) * 128, s0 + qt * 512:s0 + (qt + 1) * 512], at_b)
                                    un.fin = fin
                                units.append(un)
                load_kv(0)
                load_q(0)
                run_units(units, P, None, None)
                cx.barrier()

        def phase_attn_ab(L):
            jj = L // 2
            lam_init = 0.8 - 0.6 * math.exp(-0.3 * L)
            with ExitStack() as st:
                masks_t = sb(st, "a_m", [128, NMASK, 512], BF16)
                dma("sp", [(masks_t[:, 0:17, :], masks_in[:, 0:17, :]), (masks_t[:, 17:34, :], masks_in[:, 17:34, :])],
                    writes=[masks_t.b])
                QW = [sb(st, "a_q%d" % i, [128, 3, 512], BF16) for i in range(2)]
                KW = [[sb(st, "a_k%d_%d" % (i, v), [128, NMASK * 128], BF16) for v in range(2)] for i in range(2)]
                for kk in KW:
                    for k_ in kk:
                        op("dve", lambda q: q.memset(k_[:], 0.0), writes=[k_.b])
                VW = [sb(st, "a_v%d" % i, [128, NMASK, 192], BF16) for i in range(2)]
                P = [sb(st, "a_p%d" % i, [128, 2, 512], BF16) for i in range(3)]
                rec = [sb(st, "a_r%d" % i, [128, 512], F32) for i in range(2)]
                ot = [sb(st, "a_o%d" % i, [128, 512], BF16) for i in range(2)]
                for v in VW:
                    op("dve", lambda q: q.memset(v[:], 1.0), writes=[v.b])
                jobs = []
                for (s0, S) in seqs:
                    for jp in range(2):
                        for qt in range(S // 512):
                            jobs.append((s0, S, jp, qt))

                def windows(S, qt):
                    res = []
                    slot = 0
                    for g, d in enumerate(A_DIL):
                        lo, hi = MASK_REL[g]
                        c0 = max(0, qt * 4 + lo)
                        c1 = min(S // 128 - 1, qt * 4 + hi)
                        res.append((g, c0, c1, slot))
                        slot += c1 - c0 + 1
                    return res

                def load_job(jn):
                    s0, S, jp, qt = jobs[jn]
                    Qb, Kb, Vb = QW[jn % 2], KW[jn % 2], VW[jn % 2]
                    xq, xk, xv_ = [], [], []
                    for (g, c0, c1, slot) in windows(S, qt):
                        ch = 2 * g + jp
                        xq.append((Qb[:, g, :], qk_d[ch * 128:(ch + 1) * 128, s0 + qt * 512:s0 + (qt + 1) * 512]))
                        nck = c1 - c0 + 1
                        xk.append((Kb[0][0:64, slot * 128:(slot + nck) * 128],
                                   qk_d[768 + ch * 128:768 + ch * 128 + 64, s0 + c0 * 128:s0 + (c1 + 1) * 128]))
                        xk.append((Kb[1][64:128, slot * 128:(slot + nck) * 128],
                                   qk_d[768 + ch * 128 + 64:768 + (ch + 1) * 128, s0 + c0 * 128:s0 + (c1 + 1) * 128]))
                        vsrc = v_d[s0 + c0 * 128:s0 + (c1 + 1) * 128, :]
                        f0 = g * 256 + jp * 128
                        xv_.append((Vb[:, slot:slot + nck, 0:64], vsrc[:, f0:f0 + 64].rearrange("(c p) f -> p c f", p=128)))
                        xv_.append((Vb[:, slot:slot + nck, 128:192], vsrc[:, f0 + 64:f0 + 128].rearrange("(c p) f -> p c f", p=128)))
                    dma("sp", xq, reads=[qk_b], writes=[Qb.b])
                    dma("sp", xk, reads=[qk_b], writes=[Kb[0].b, Kb[1].b])
                    dma("sp", xv_, reads=[v_b], writes=[Vb.b])

                units = []
                for jn, (s0, S, jp, qt) in enumerate(jobs):
                    Qb, Kb, Vb = QW[jn % 2], KW[jn % 2], VW[jn % 2]
                    par = jn % 2
                    acc = (4 + 2 * par, 5 + 2 * par)
                    wins = windows(S, qt)
                    ulist = []
                    for hb in range(2):
                        for (g, c0, c1, slot) in wins:
                            c = c0
                            while c <= c1:
                                nb = 2 if c + 1 <= c1 else 1
                                ulist.append((hb, g, c, nb, slot + (c - c0)))
                                c += nb
                    first = {0: True, 1: True}
                    lastidx = {}
                    for ui, (hb, g, c, nb, sl) in enumerate(ulist):
                        lastidx[hb] = ui
                    for ui, (hb, g, c, nb, sl) in enumerate(ulist):
                        un = Unit()
                        un.pre = None
                        if ui == 0:
                            def pre(jn=jn):
                                if jn + 1 < len(jobs):
                                    load_job(jn + 1)
                            un.pre = pre
                        un.qk = [(Kb[hb][:, (sl + e) * 128:(sl + e + 1) * 128], Qb[:, g, :]) for e in range(nb)]
                        un.nb = nb
                        lo = MASK_REL[g][0]
                        un.mask = MASK_BASE[g] + (c - qt * 4) - lo
                        un.rd = [Kb[hb].b, Qb.b]
                        vs = slice(0, 128) if hb == 0 else slice(64, 192)
                        un.pv = []
                        for e in range(nb):
                            un.pv.append((acc[hb], Vb[:, sl + e, vs], e, first[hb], (ui == lastidx[hb] and e == nb - 1), [Vb.b]))
                            first[hb] = False
                        un.fin = None
                        if ui == len(ulist) - 1:
                            def fin(acc=acc, par=par, jp=jp, s0=s0, qt=qt):
                                fin_pair(acc[0], acc[1], rec[par], ot[par],
                                         at_d[jp * 128:(jp + 1) * 128, s0 + qt * 512:s0 + (qt + 1) * 512], at_b)
                            un.fin = fin
                        units.append(un)
                bg = []
                if deferred:
                    dci = [sb(st, "dc_i%d" % i, [128, 2048], F32) for i in range(2)]
                    dco = [sb(st, "dc_o%d" % i, [128, 2048], BF16) for i in range(2)]

                    def mk(k, d2, s2, w):
                        def job():
                            ci, co = dci[k % 2], dco[k % 2]
                            dma("sp", [(ci[:, 0:w], s2)], writes=[ci.b])
                            op("dve", lambda q: q.tensor_copy(out=co[:, 0:w], in_=ci[:, 0:w]), reads=[ci.b], writes=[co.b])
                            dma("sp", [(d2, co[:, 0:w])], reads=[co.b], writes=[wts_b])
                        return job
                    for k, (d2, s2, w) in enumerate(deferred):
                        bg.append(mk(k, d2, s2, w))
                    for L_ in deferred_rearr:
                        bg.append(lambda L_=L_: rearr_wup(L_))
                    del deferred[:]
                    del deferred_rearr[:]
                load_job(0)
                run_units(units, P, None, masks_t, bg=bg, bg_every=max(1, (len(units) - 8) // max(1, len(bg))))
                cx.barrier()

            with ExitStack() as st, nc.allow_non_contiguous_dma(reason="small parameter vectors"):
                lv = sb(st, "b_lv", [128, 4, 64], F32)
                for col, src in enumerate((lq1, lk1, lq2, lk2)):
                    dma("sp", [(lv[:, col, :], src[jj].partition_broadcast(128))], writes=[lv.b])
                lp = sb(st, "b_lp", [128, 2, 64], F32)
                ls = sb(st, "b_ls", [128, 4], F32)
                op("dve", lambda q: q.tensor_tensor(out=lp[:, 0, :], in0=lv[:, 0, :], in1=lv[:, 1, :], op=ALU.mult),
                   reads=[lv.b], writes=[lp.b])
                op("dve", lambda q: q.tensor_tensor(out=lp[:, 1, :], in0=lv[:, 2, :], in1=lv[:, 3, :], op=ALU.mult),
                   reads=[lv.b], writes=[lp.b])
                op("dve", lambda q: q.tensor_reduce(out=ls[:, 0:2], in_=lp[:], axis=mybir.AxisListType.X, op=ALU.add),
                   reads=[lp.b], writes=[ls.b])
                op("act", lambda q: q.activation(out=ls[:, 2:4], in_=ls[:, 0:2], func=AF.Exp), reads=[ls.b], writes=[ls.b])
                nlam = sb(st, "b_nl", [128, 1], F32)
                op("dve", lambda q: q.scalar_tensor_tensor(out=nlam[:], in0=ls[:, 3:4], scalar=-lam_init, in1=ls[:, 2:3],
                                                           op0=ALU.add, op1=ALU.subtract),
                   reads=[ls.b], writes=[nlam.b])
                gs = sb(st, "b_gs", [128, 1], F32)
                dma("sp", [(gs[:], subln[jj].rearrange("(p o) -> p o", o=1))], writes=[gs.b])
                op("dve", lambda q: q.tensor_scalar(out=gs[:], in0=gs[:], scalar1=1.0 - lam_init, scalar2=None, op0=ALU.mult),
                   reads=[gs.b], writes=[gs.b])

                KT = [[sb(st, "b_k%d_%d" % (i, v), [128, TABLE_S], BF16) for v in range(2)] for i in range(2)]
                for kk in KT:
                    for k_ in kk:
                        op("dve", lambda q: q.memset(k_[:], 0.0), writes=[k_.b])
                VV = [sb(st, "b_v%d" % i, [128, TABLE_S // 128, 128], BF16) for i in range(2)]
                QT = [sb(st, "b_q%d" % i, [128, TABLE_S], BF16) for i in range(2)]
                P = [sb(st, "b_p%d" % i, [128, 2, 512], BF16) for i in range(3)]
                Psm = [sb(st, "b_ps%d" % i, [128, 512], BF16) for i in range(3)]
                r0t = sb(st, "b_r0", [128, 512], F32)
                r1t = sb(st, "b_r1", [128, 512], F32)
                u0t = sb(st, "b_u0", [128, 512], F32)
                u1t = sb(st, "b_u1", [128, 512], F32)
                ob = sb(st, "b_ob", [128, 512], F32)
                osq = sb(st, "b_sq", [128, 512], BF16)
                oln = sb(st, "b_ln", [128, 512], F32)
                ors = sb(st, "b_rs", [128, 512], F32)
                ot = [sb(st, "b_o%d" % i, [128, 512], BF16) for i in range(2)]
                jobs = []
                for (s0, S) in seqs:
                    for h in range(4):
                        jobs.append((s0, S, h))

                def load_job(jn):
                    s0, S, h = jobs[jn]
                    dma("sp", [(KT[jn % 2][0][0:64, 0:S], qk_d[2048 + h * 128:2048 + h * 128 + 64, s0:s0 + S]),
                               (KT[jn % 2][1][64:128, 0:S], qk_d[2048 + h * 128 + 64:2048 + (h + 1) * 128, s0:s0 + S])],
                        reads=[qk_b], writes=[KT[jn % 2][0].b, KT[jn % 2][1].b])
                    dma("sp", [(QT[jn % 2][:, 0:S], qk_d[1536 + h * 128:1536 + (h + 1) * 128, s0:s0 + S])],
                        reads=[qk_b], writes=[QT[jn % 2].b])
                    dma("sp", [(VV[jn % 2][:, 0:S // 128, :],
                                v_d[s0:s0 + S, 768 + h * 128:768 + (h + 1) * 128].rearrange("(c p) f -> p c f", p=128))],
                        reads=[v_b], writes=[VV[jn % 2].b])

                units = []
                lazy = []
                fc = [0]
                for jn, (s0, S, h) in enumerate(jobs):
                    K_, V_, Q_ = KT[jn % 2], VV[jn % 2], QT[jn % 2]
                    nkp = S // 256
                    for qt in range(S // 512):
                        par = fc[0] % 2
                        fc[0] += 1
                        for kp in range(nkp):
                            for m in range(2):
                                un = Unit()
                                un.pre = None
                                if qt == 0 and kp == 0 and m == 0:
                                    def pre(jn=jn):
                                        if jn + 1 < len(jobs):
                                            load_job(jn + 1)
                                    un.pre = pre
                                un.qk = [(K_[m][:, (2 * kp + e) * 128:(2 * kp + e + 1) * 128],
                                          Q_[:, qt * 512:(qt + 1) * 512]) for e in range(2)]
                                un.nb = 2
                                un.mask = None
                                un.rd = [K_[m].b, Q_.b]
                                un.pv = []
                                for e in range(2):
                                    st_ = (kp == 0 and e == 0)
                                    sp_ = (kp == nkp - 1 and e == 1)
                                    un.pv.append((4 + 2 * m, V_[:, 2 * kp + e, :], e, st_, sp_, [V_.b]))
                                un.den = (5 + 2 * m, kp == 0, kp == nkp - 1)
                                un.fin = None
                                if kp == nkp - 1 and m == 1:
                                    def fin(par=par, h=h, s0=s0, qt=qt):
                                        while lazy:
                                            lazy.pop(0)()
                                        o_ = ot[par]
                                        for bk_, dst_ in ((4, u0t), (5, r0t), (6, u1t), (7, r1t)):
                                            op("dve", lambda q: q.tensor_copy(out=dst_[:], in_=ps[:, bk_, :]),
                                               reads=[bank_b[bk_]], writes=[dst_.b])

                                        def rc(t_, lo_, hi_):
                                            return lambda: op("dve", lambda q: q.reciprocal(out=t_[:, lo_:hi_], in_=t_[:, lo_:hi_]),
                                                              reads=[t_.b], writes=[t_.b])

                                        def ml(u_, r_):
                                            return lambda: op("dve", lambda q: q.tensor_tensor(out=u_[:], in0=u_[:], in1=r_[:], op=ALU.mult),
                                                              reads=[u_.b, r_.b], writes=[u_.b])

                                        def mid():
                                            op("dve", lambda q: q.scalar_tensor_tensor(out=ob[:], in0=u1t[:], scalar=nlam[:, 0:1], in1=u0t[:],
                                                                                       op0=ALU.mult, op1=ALU.add),
                                               reads=[u1t.b, u0t.b, nlam.b], writes=[ob.b])
                                            op("act", lambda q: q.activation(out=osq[:], in_=ob[:], func=AF.Square), reads=[ob.b], writes=[osq.b])
                                            op("pe", lambda q: q.matmul(ps[:, 0, :], ones_m, osq[:], start=True, stop=True),
                                               reads=[osq.b, cmats.b], writes=[bank_b[0]])
                                            op("act", lambda q: q.activation(out=oln[:], in_=ps[:, 0, :], func=AF.Ln, bias=cst[:, 1:2], scale=1.0 / 128),
                                               reads=[bank_b[0], cst.b], writes=[oln.b])
                                            op("act", lambda q: q.activation(out=ors[:], in_=oln[:], func=AF.Exp, scale=-0.5), reads=[oln.b], writes=[ors.b])

                                        def last(o_=o_, h=h, s0=s0, qt=qt):
                                            op("dve", lambda q: q.scalar_tensor_tensor(out=o_[:], in0=ob[:], scalar=gs[:, 0:1], in1=ors[:],
                                                                                       op0=ALU.mult, op1=ALU.mult),
                                               reads=[ob.b, gs.b, ors.b], writes=[o_.b])
                                            dma("sp", [(at_d[256 + h * 128:256 + (h + 1) * 128, s0 + qt * 512:s0 + (qt + 1) * 512], o_[:])],
                                                reads=[o_.b], writes=[at_b])

                                        lazy.extend([rc(r0t, 0, 256), rc(r0t, 256, 512), ml(u0t, r0t),
                                                     rc(r1t, 0, 256), rc(r1t, 256, 512), ml(u1t, r1t), mid, last])
                                    un.fin = fin
                                units.append(un)
                load_job(0)
                run_units(units, P, Psm, None, lazy=lazy)
                cx.barrier()

        def phase_ffn(L, cur):
            even = (L % 2 == 0)
            jj = L // 2
            nfc = 6 if even else 8
            Wod = wout_ab_d[jj] if even else wout_c_d[jj]
            with ExitStack() as st, nc.allow_non_contiguous_dma(reason="small parameter vectors"):
                Wo = sb(st, "f_wo", [128, nfc, D], BF16)
                dma("sp", [(Wo[:], Wod.rearrange("(k p) n -> p k n", p=128))], reads=[wts_b], writes=[Wo.b])
                Wdn = sb(st, "f_wd", [128, NPAIR, D], BF16)
                dma("sp", [(Wdn[:, 0:11, :], wdown_d[L, 0:11 * 128, :].rearrange("(k p) n -> p k n", p=128)),
                           (Wdn[:, 11:22, :], wdown_d[L, 11 * 128:22 * 128, :].rearrange("(k p) n -> p k n", p=128))],
                    reads=[wts_b], writes=[Wdn.b])
                gf = sb(st, "f_g", [128, 8], F32)
                dma("sp", [(gf[:], norm_ffn[L].rearrange("(c p) -> p c", p=128))], writes=[gf.b])
                cw = sb(st, "f_cw", [128, 3, 2 * NPAIR], F32)
                dma("sp", [(cw[:, t, :], conv_w[L, t].rearrange("(m p) -> p m", p=128)) for t in range(3)], writes=[cw.b])
                cb = sb(st, "f_cb", [128, 2 * NPAIR], F32)
                dma("sp", [(cb[:], conv_b[L].rearrange("(m p) -> p m", p=128))], writes=[cb.b])
                NW = 3
                Wu = [sb(st, "f_wu%d" % i, [128, 8, 2, 128], BF16) for i in range(NW)]
                xt = [sb(st, "f_x%d" % i, [128, 8, 512], F32) for i in range(2)]
                att = [sb(st, "f_a%d" % i, [128, nfc, 512], BF16) for i in range(2)]
                x1 = sb(st, "f_x1", [128, 8, 512], F32)
                x2 = sb(st, "f_x2", [128, 8, 512], F32)
                sq = sb(st, "f_sq", [128, 8, 512], BF16)
                hT = sb(st, "f_h", [128, 8, 512], BF16)
                lnv = sb(st, "f_ln", [128, 512], F32)
                rstd = sb(st, "f_rs", [128, 512], F32)
                gT = sb(st, "f_gT", [128, NPAIR, 512], BF16)
                ca = [sb(st, "f_ca%d" % i, [128, 512], F32) for i in range(2)]
                cbv = [sb(st, "f_cv%d" % i, [128, 512], F32) for i in range(2)]
                xr = xT[cur].rearrange("(c p) t -> p c t", p=128)
                xw = xT[1 - cur].rearrange("(c p) t -> p c t", p=128)
                atv = at_d.rearrange("(c p) t -> p c t", p=128)

                tiles = []
                for (s0, S) in seqs:
                    o0 = 0
                    while o0 < S:
                        o1 = min(S, o0 + 496)
                        lo = max(0, o0 - 1)
                        hi = min(S, o1 + 1)
                        tiles.append((s0, S, lo, hi, o0, o1))
                        o0 = o1
                nt = len(tiles)
                wu_ctr = [0]

                def load_x(i):
                    s0, S, lo, hi, o0, o1 = tiles[i]
                    n = hi - lo
                    dma("sp", [(xt[i % 2][:, :, 0:n], xr[:, :, s0 + lo:s0 + hi])], reads=[xT_b[cur]], writes=[xt[i % 2].b])
                    dma("sp", [(att[i % 2][:, :, 0:n], atv[:, 0:nfc, s0 + lo:s0 + hi])], reads=[at_b], writes=[att[i % 2].b])

                def load_wu(idx):
                    m = idx % NPAIR
                    w_ = Wu[idx % NW]
                    dma("sp", [(w_[:], wup_p[L, m])], reads=[wupp_b], writes=[w_.b])

                total_pieces = nt * NPAIR
                for idx in range(min(NW - 1, total_pieces)):
                    load_wu(idx)
                load_x(0)
                for i in range(nt):
                    s0, S, lo, hi, o0, o1 = tiles[i]
                    n = hi - lo
                    if i + 1 < nt:
                        load_x(i + 1)
                    x_, a_ = xt[i % 2], att[i % 2]
                    for c in range(8):
                        bk = 4 + c % 2
                        for k in range(nfc):
                            op("pe", lambda q: q.matmul(ps[:, bk, 0:n], Wo[:, k, c * 128:(c + 1) * 128], a_[:, k, 0:n],
                                                        start=(k == 0), stop=(k == nfc - 1)),
                               reads=[Wo.b, a_.b], writes=[bank_b[bk]])
                        op("dve", lambda q: q.tensor_tensor(out=x1[:, c, 0:n], in0=ps[:, bk, 0:n], in1=x_[:, c, 0:n], op=ALU.add),
                           reads=[bank_b[bk], x_.b], writes=[x1.b])
                    rms_stats(x1, n, sq, 6, lnv, rstd, 0, 1.0 / D)
                    for c in range(8):
                        op("dve", lambda q: q.scalar_tensor_tensor(out=hT[:, c, 0:n], in0=x1[:, c, 0:n], scalar=gf[:, c:c + 1],
                                                                   in1=rstd[:, 0:n], op0=ALU.mult, op1=ALU.mult),
                           reads=[x1.b, gf.b, rstd.b], writes=[hT.b])
                    left_pad = (lo == 0)
                    right_pad = (hi == S)

                    def conv(bank, dst, m, half):
                        col = half * NPAIR + m
                        op("act", lambda q: q.activation(out=dst[:, 0:n], in_=ps[:, bank, 0:n], func=AF.Identity,
                                                         bias=cb[:, col:col + 1], scale=cw[:, 1, col:col + 1]),
                           reads=[bank_b[bank], cb.b, cw.b], writes=[dst.b])
                        op("dve", lambda q: q.scalar_tensor_tensor(out=dst[:, 1:n], in0=ps[:, bank, 0:n - 1],
                                                                   scalar=cw[:, 0, col:col + 1], in1=dst[:, 1:n],
                                                                   op0=ALU.mult, op1=ALU.add),
                           reads=[bank_b[bank], cw.b, dst.b], writes=[dst.b])
                        op("dve", lambda q: q.scalar_tensor_tensor(out=dst[:, 0:n - 1], in0=ps[:, bank, 1:n],
                                                                   scalar=cw[:, 2, col:col + 1], in1=dst[:, 0:n - 1],
                                                                   op0=ALU.mult, op1=ALU.add),
                           reads=[bank_b[bank], cw.b, dst.b], writes=[dst.b])

                    def U1(m):
                        idx = wu_ctr[0]
                        wu_ctr[0] += 1
                        if idx + NW - 1 < total_pieces:
                            load_wu(idx + NW - 1)
                        w_ = Wu[idx % NW]
                        for t in range(2):
                            bk = (m % 2) * 2 + t
                            for k in range(8):
                                op("pe", lambda q: q.matmul(ps[:, bk, 0:n], w_[:, k, t, :], hT[:, k, 0:n],
                                                            start=(k == 0), stop=(k == 7)),
                                   reads=[w_.b, hT.b], writes=[bank_b[bk]])

                    def U2(m):
                        b0 = (m % 2) * 2
                        conv(b0, ca[m % 2], m, 0)
                        conv(b0 + 1, cbv[m % 2], m, 1)
                        op("act", lambda q: q.activation(out=ca[m % 2][:, 0:n], in_=ca[m % 2][:, 0:n], func=AF.Silu),
                           reads=[ca[m % 2].b], writes=[ca[m % 2].b])
                        op("dve", lambda q: q.tensor_tensor(out=gT[:, m, 0:n], in0=ca[m % 2][:, 0:n], in1=cbv[m % 2][:, 0:n], op=ALU.mult),
                           reads=[ca[m % 2].b, cbv[m % 2].b], writes=[gT.b])

                    for m in range(NPAIR + 1):
                        if m < NPAIR:
                            U1(m)
                        if m >= 1:
                            U2(m - 1)
                    for c in range(8):
                        bk = 4 + c % 2
                        for m in range(NPAIR):
                            op("pe", lambda q: q.matmul(ps[:, bk, 0:n], Wdn[:, m, c * 128:(c + 1) * 128], gT[:, m, 0:n],
                                                        start=(m == 0), stop=(m == NPAIR - 1)),
                               reads=[Wdn.b, gT.b], writes=[bank_b[bk]])
                        op("dve", lambda q: q.tensor_tensor(out=x2[:, c, 0:n], in0=ps[:, bk, 0:n], in1=x1[:, c, 0:n], op=ALU.add),
                           reads=[bank_b[bk], x1.b], writes=[x2.b])
                    a0 = o0 - lo
                    a1 = o1 - lo
                    dma("sp", [(xw[:, :, s0 + o0:s0 + o1], x2[:, :, a0:a1])], reads=[x2.b], writes=[xT_b[1 - cur]])
                cx.barrier()

        phase_transpose_in()
        cur = 0
        for L in range(depth):
            phase_in_proj(L, cur)
            if L % 2 == 0:
                phase_attn_ab(L)
            else:
                phase_attn_c(L)
            phase_ffn(L, cur)
            cur = 1 - cur
        phase_transpose_out(cur)
        if debug:
            dq = nc.dram_tensor("dbg_qk", [2560, TOK], BF16, kind="ExternalOutput").ap()
            dv = nc.dram_tensor("dbg_v", [TOK, 1280], BF16, kind="ExternalOutput").ap()
            da = nc.dram_tensor("dbg_at", [D, TOK], BF16, kind="ExternalOutput").ap()
            dma("sp", [(dq[:, :], qk_d[:, :]), (dv[:, :], v_d[:, :]), (da[:, :], at_d[:, :])], reads=[qk_b, v_b, at_b], writes=[Buf()])
            cx.barrier()
    return nc


_CONST_CACHE = {}


def _consts():
    if "c" not in _CONST_CACHE:
        _CONST_CACHE["c"] = host_constants()
    return _CONST_CACHE["c"]


WEIGHT_KEYS = ["norm_mix", "norm_ffn", "w_in_ab", "q_norm_a", "k_norm_a", "q_norm_b", "k_norm_b",
               "lambda_q1", "lambda_k1", "lambda_q2", "lambda_k2", "subln_b", "w_out_ab",
               "w_in_c", "q_norm_c", "k_norm_c", "w_out_c", "w_up", "conv_w", "conv_b", "w_down"]


def kernel(**inputs):
    n = 8
    xp = np.asarray(inputs["x_prompt"], dtype=np.float32)
    xs = np.asarray(inputs["x_sample"], dtype=np.float32)
    B, S, _ = xp.shape
    DB, DS, _ = xs.shape
    pp = B // n
    sp = DB // n
    seq_lens = [S] * pp + [DS] * sp
    depth = int(np.asarray(inputs["norm_mix"]).shape[0])
    nc = build_program(seq_lens, depth)
    tabs, cm, masks = _consts()
    shared = {k: np.ascontiguousarray(np.asarray(inputs[k], dtype=np.float32)) for k in WEIGHT_KEYS}
    shared["tabs"] = tabs
    shared["cmats"] = cm
    shared["ident"] = np.eye(128, dtype=np.float32)
    shared["masks"] = masks
    in_maps = []
    for c in range(n):
        xa = np.concatenate([xp[c * pp:(c + 1) * pp].reshape(-1, D), xs[c * sp:(c + 1) * sp].reshape(-1, D)], axis=0)
        m = dict(shared)
        m["x_all"] = np.ascontiguousarray(xa)
        in_maps.append(m)
    res = run_bass_kernel_spmd(nc, in_maps, core_ids=list(range(n)))
    yp = np.empty((B, S, D), np.float32)
    ys = np.empty((DB, DS, D), np.float32)
    for c in range(n):
        y = np.asarray(res.results[c]["y_all"], dtype=np.float32)
        yp[c * pp:(c + 1) * pp] = y[:pp * S].reshape(pp, S, D)
        ys[c * sp:(c + 1) * sp] = y[pp * S:].reshape(sp, DS, D)
    return (yp, ys)
```

```python
import math
from contextlib import ExitStack

import numpy as np
import ml_dtypes

import concourse.bass as bass
import concourse.mybir as mybir
from concourse.bass_utils import run_bass_kernel_spmd

F32 = mybir.dt.float32
BF16 = mybir.dt.bfloat16
AF = mybir.ActivationFunctionType
ALU = mybir.AluOpType

D = 1024
DFF = 2816
NPAIR = DFF // 128
A_DIL = (1, 4, 16)
MASK_REL = {0: (-1, 4), 1: (-2, 5), 2: (-8, 11)}
MASK_BASE = {0: 0, 1: 6, 2: 14}
NMASK = 34
SCALE = 0.125
TABLE_S = 8192


class Buf:
    __slots__ = ("w", "r")

    def __init__(self):
        self.w = {}
        self.r = {}


class TT:
    def __init__(self, t):
        self.t = t
        self.b = Buf()

    def __getitem__(self, idx):
        return self.t[idx]


class Eng:
    def __init__(self, name, q, sem):
        self.name = name
        self.q = q
        self.sem = sem
        self.count = 0
        self.seen = {}


class DSlot:
    def __init__(self, sem, key):
        self.sem = sem
        self.key = key
        self.issued = 0


def _merge(d, s):
    for k, v in s.items():
        if d.get(k, 0) < v:
            d[k] = v


class Ctx:
    def __init__(self, nc, stack, nslots=8):
        self.nc = nc
        self.stack = stack
        self.E = {}
        self.sems = {}
        for name, q in (("pe", nc.tensor), ("act", nc.scalar), ("dve", nc.vector),
                        ("pool", nc.gpsimd), ("sp", nc.sync)):
            sem = stack.enter_context(nc.semaphore("sem_" + name))
            self.E[name] = Eng(name, q, sem)
            self.sems[name] = sem
        self.slots = {}
        self.nexts = {}
        for qn, n in (("sp", nslots), ("pool", 4)):
            lst = []
            for i in range(n):
                key = "d_%s%d" % (qn, i)
                sem = stack.enter_context(nc.semaphore(key))
                self.sems[key] = sem
                lst.append(DSlot(sem, key))
            self.slots[qn] = lst
            self.nexts[qn] = 0

    def _wait(self, E, deps):
        for key, val in deps.items():
            if key == E.name:
                continue
            if E.seen.get(key, 0) >= val:
                continue
            E.q.wait_ge(self.sems[key], val)
            E.seen[key] = val

    def op(self, en, fn, reads=(), writes=()):
        E = self.E[en]
        deps = {}
        for b in reads:
            _merge(deps, b.w)
        for b in writes:
            _merge(deps, b.w)
            _merge(deps, b.r)
        self._wait(E, deps)
        ins = fn(E.q)
        E.count += 1
        ins.then_inc(E.sem, 1)
        c = E.count
        for b in reads:
            b.r[en] = c
        for b in writes:
            b.w[en] = c
        return ins

    def dma(self, en, xfers, reads=(), writes=(), **kw):
        E = self.E[en]
        deps = {}
        for b in reads:
            _merge(deps, b.w)
        for b in writes:
            _merge(deps, b.w)
            _merge(deps, b.r)
        self._wait(E, deps)
        lst = self.slots[en]
        slot = lst[self.nexts[en]]
        self.nexts[en] = (self.nexts[en] + 1) % len(lst)
        if E.seen.get(slot.key, 0) < slot.issued:
            E.q.wait_ge(slot.sem, slot.issued)
            E.seen[slot.key] = slot.issued
        for (o, i) in xfers:
            E.q.dma_start(out=o, in_=i, **kw).then_inc(slot.sem, 16)
            slot.issued += 16
        for b in reads:
            b.r[slot.key] = slot.issued
        for b in writes:
            b.w[slot.key] = slot.issued

    def barrier(self):
        deps = {}
        for name, E in self.E.items():
            if E.count:
                deps[name] = E.count
        for lst in self.slots.values():
            for s in lst:
                if s.issued:
                    deps[s.key] = s.issued
        for name in ("pe", "act", "dve", "sp"):
            self._wait(self.E[name], deps)


def _rope(pos, dim, theta):
    inv = (np.float32(theta) ** (-np.arange(0, dim, 2, dtype=np.float32) / np.float32(dim))).astype(np.float32)
    ang = pos.astype(np.float32)[:, None] * inv[None, :]
    return np.cos(ang).astype(np.float32), np.sin(ang).astype(np.float32)


def host_constants():
    S = TABLE_S
    tabs = np.zeros((2, 2, 128, S), np.float32)
    perms = np.zeros((2, 128, 128), np.float32)
    pos = np.arange(S)
    cos, sin = _rope(pos, 16, 500000.0)
    cr, sr = _rope(pos // 64, 32, 10000.0)
    cc, sc = _rope(pos % 64, 32, 10000.0)
    for p in range(128):
        d = p % 64
        hb = p - d
        if d < 8:
            tabs[0, 0, p] = cos[:, d]
            tabs[0, 1, p] = -sin[:, d]
            perms[0, hb + d + 8, p] = 1.0
        elif d < 16:
            tabs[0, 0, p] = cos[:, d - 8]
            tabs[0, 1, p] = sin[:, d - 8]
            perms[0, hb + d - 8, p] = 1.0
        else:
            tabs[0, 0, p] = 1.0
        if d < 16:
            tabs[1, 0, p] = cr[:, d]
            tabs[1, 1, p] = -sr[:, d]
            perms[1, hb + d + 16, p] = 1.0
        elif d < 32:
            tabs[1, 0, p] = cr[:, d - 16]
            tabs[1, 1, p] = sr[:, d - 16]
            perms[1, hb + d - 16, p] = 1.0
        elif d < 48:
            tabs[1, 0, p] = cc[:, d - 32]
            tabs[1, 1, p] = -sc[:, d - 32]
            perms[1, hb + d + 16, p] = 1.0
        else:
            tabs[1, 0, p] = cc[:, d - 48]
            tabs[1, 1, p] = sc[:, d - 48]
            perms[1, hb + d - 16, p] = 1.0
    cm = np.zeros((128, 5, 128), np.float32)
    cm[:, 0] = np.eye(128, dtype=np.float32)
    cm[:, 1] = 1.0
    cm[0:64, 2, 0:64] = 1.0
    cm[64:128, 2, 64:128] = 1.0
    cm[:, 3] = perms[0]
    cm[:, 4] = perms[1]
    masks = np.zeros((128, NMASK, 512), np.float32)
    i = np.arange(128)[:, None]
    j = np.arange(512)[None, :]
    for g, d in enumerate(A_DIL):
        lo, hi = MASK_REL[g]
        for rel in range(lo, hi + 1):
            diff = rel * 128 + i - j
            ok = (np.abs(diff) <= 64 * d) & (diff % d == 0)
            masks[:, MASK_BASE[g] + rel - lo, :] = (ok.astype(np.float32) - 1.0) * 30000.0
    return tabs, cm.astype(ml_dtypes.bfloat16), masks.astype(ml_dtypes.bfloat16)


def build_program(seq_lens, depth=4, debug=False):
    TOK = sum(seq_lens)
    seqs = []
    o = 0
    for S in seq_lens:
        seqs.append((o, S))
        o += S
    n_even = (depth + 1) // 2
    n_odd = depth // 2

    nc = bass.Bass("TRN2", target_bir_lowering=False)

    def din(name, shape, dt=F32):
        return nc.dram_tensor(name, list(shape), dt, kind="ExternalInput").ap()

    def dscr(name, shape, dt):
        return nc.dram_tensor(name, list(shape), dt, kind="Internal").ap()

    x_all = din("x_all", [TOK, D])
    y_all = nc.dram_tensor("y_all", [TOK, D], F32, kind="ExternalOutput").ap()
    norm_mix = din("norm_mix", [depth, D])
    norm_ffn = din("norm_ffn", [depth, D])
    w_in_ab = din("w_in_ab", [n_even, D, 3840])
    qn_a = din("q_norm_a", [n_even, 64])
    kn_a = din("k_norm_a", [n_even, 64])
    qn_b = din("q_norm_b", [n_even, 64])
    kn_b = din("k_norm_b", [n_even, 64])
    lq1 = din("lambda_q1", [n_even, 64])
    lk1 = din("lambda_k1", [n_even, 64])
    lq2 = din("lambda_q2", [n_even, 64])
    lk2 = din("lambda_k2", [n_even, 64])
    subln = din("subln_b", [n_even, 128])
    w_out_ab = din("w_out_ab", [n_even, 768, D])
    w_in_c = din("w_in_c", [max(n_odd, 1), D, 1536])
    qn_c = din("q_norm_c", [max(n_odd, 1), 64])
    kn_c = din("k_norm_c", [max(n_odd, 1), 64])
    w_out_c = din("w_out_c", [max(n_odd, 1), D, D])
    w_up = din("w_up", [depth, D, 2 * DFF])
    conv_w = din("conv_w", [depth, 3, 2 * DFF])
    conv_b = din("conv_b", [depth, 2 * DFF])
    w_down = din("w_down", [depth, DFF, D])
    tabs_in = din("tabs", [2, 2, 128, TABLE_S])
    cm_in = din("cmats", [128, 5, 128], BF16)
    ident_in = din("ident", [128, 128])
    masks_in = din("masks", [128, NMASK, 512], BF16)

    xT = [dscr("xT0", [D, TOK], F32), dscr("xT1", [D, TOK], F32)]
    qk_d = dscr("qk_d", [2560, TOK], BF16)
    v_d = dscr("v_d", [TOK, 1280], BF16)
    at_d = dscr("at_d", [D, TOK], BF16)
    win_ab_d = dscr("win_ab_d", [n_even, D, 3840], BF16)
    wout_ab_d = dscr("wout_ab_d", [n_even, 768, D], BF16)
    win_c_d = dscr("win_c_d", [max(n_odd, 1), D, 1536], BF16)
    wout_c_d = dscr("wout_c_d", [max(n_odd, 1), D, D], BF16)
    wdown_d = dscr("wdown_d", [depth, DFF, D], BF16)
    wup_nat = dscr("wup_nat", [depth, D, 2 * DFF], BF16)
    wup_p = dscr("wup_p", [depth, NPAIR, 128, 8, 2, 128], BF16)

    with ExitStack() as top:
        cx = Ctx(nc, top)
        op, dma = cx.op, cx.dma

        uid = [0]

        def sb(stack, name, shape, dt):
            uid[0] += 1
            return TT(stack.enter_context(nc.sbuf_tensor("%s_u%d" % (name, uid[0]), list(shape), dt)))

        ps = top.enter_context(nc.psum_tensor("ps", [128, 8, 512], F32))
        bank_b = [Buf() for _ in range(8)]

        cmats = sb(top, "cmats", [128, 5, 128], BF16)
        ident_f = sb(top, "ident_f", [128, 128], F32)
        cst = sb(top, "cst", [128, 4], F32)
        dma("sp", [(cmats[:], cm_in[:, :, :])], writes=[cmats.b])
        dma("sp", [(ident_f[:], ident_in[:, :])], writes=[ident_f.b])
        op("dve", lambda q: q.memset(cst[:, 0:1], 1e-6), writes=[cst.b])
        op("dve", lambda q: q.memset(cst[:, 1:2], 1e-5), writes=[cst.b])
        ones_m = cmats[:, 1, :]
        blk_m = cmats[:, 2, :]

        xT_b = [Buf(), Buf()]
        qk_b, v_b, at_b = Buf(), Buf(), Buf()
        wts_b = Buf()

        cstack = ExitStack()
        cin = [sb(cstack, "cast_i%d" % i, [128, 4096], F32) for i in range(2)]
        cout = [sb(cstack, "cast_o%d" % i, [128, 4096], BF16) for i in range(2)]
        cctr = [0]

        def cast_flat(dst, src, n_elems):
            x = n_elems // 128
            assert x * 128 == n_elems
            s2 = src.rearrange("(p x) -> p x", p=128)
            d2 = dst.rearrange("(p x) -> p x", p=128)
            c0 = 0
            while c0 < x:
                c1 = min(x, c0 + 4096)
                w = c1 - c0
                i = cctr[0] % 2
                cctr[0] += 1
                ci, co = cin[i], cout[i]
                dma("sp", [(ci[:, 0:w], s2[:, c0:c1])], writes=[ci.b])
                if i == 0:
                    op("dve", lambda q: q.tensor_copy(out=co[:, 0:w], in_=ci[:, 0:w]), reads=[ci.b], writes=[co.b])
                else:
                    op("act", lambda q: q.activation(out=co[:, 0:w], in_=ci[:, 0:w], func=AF.Copy), reads=[ci.b], writes=[co.b])
                dma("sp", [(d2[:, c0:c1], co[:, 0:w])], reads=[co.b], writes=[wts_b])
                c0 = c1

        def flat2(ap2):
            return ap2.rearrange("b c -> (b c)")

        deferred = []
        deferred_rearr = []

        def cast_later(dst, src, n_elems):
            x = n_elems // 128
            s2 = src.rearrange("(p x) -> p x", p=128)
            d2 = dst.rearrange("(p x) -> p x", p=128)
            c0 = 0
            while c0 < x:
                c1 = min(x, c0 + 2048)
                deferred.append((d2[:, c0:c1], s2[:, c0:c1], c1 - c0))
                c0 = c1

        def rearr_wup(L):
            for m in range(NPAIR):
                xf = []
                for t in range(2):
                    src = wup_nat[L, :, t * DFF + m * 128: t * DFF + (m + 1) * 128].rearrange("(k p) c -> p k c", p=128)
                    xf.append((wup_p[L, m, :, :, t, :], src))
                dma("sp", xf, reads=[wts_b], writes=[wupp_b])

        wupp_b = Buf()
        for L in range(depth):
            fn = cast_flat if L == 0 else cast_later
            jj_ = L // 2
            if L % 2 == 0:
                fn(flat2(win_ab_d[jj_]), flat2(w_in_ab[jj_]), D * 3840)
                fn(flat2(wout_ab_d[jj_]), flat2(w_out_ab[jj_]), 768 * D)
            else:
                fn(flat2(win_c_d[jj_]), flat2(w_in_c[jj_]), D * 1536)
                fn(flat2(wout_c_d[jj_]), flat2(w_out_c[jj_]), D * D)
            fn(flat2(wdown_d[L]), flat2(w_down[L]), DFF * D)
            fn(flat2(wup_nat[L]), flat2(w_up[L]), D * 2 * DFF)
            if L == 0:
                rearr_wup(0)
            else:
                deferred_rearr.append(L)
        cx.barrier()
        cstack.close()

        def phase_transpose_in():
            with ExitStack() as st:
                xin = [sb(st, "p0_x%d" % i, [128, 4, D], F32) for i in range(2)]
                xo = [sb(st, "p0_o%d" % i, [128, 8, 512], F32) for i in range(2)]
                ntile = TOK // 512
                xv = xT[0].rearrange("(c p) t -> p c t", p=128)

                def load(i):
                    T0 = i * 512
                    dma("sp", [(xin[i % 2][:], x_all[T0:T0 + 512, :].rearrange("(j p) f -> p j f", p=128))],
                        writes=[xin[i % 2].b])

                load(0)
                for i in range(ntile):
                    if i + 1 < ntile:
                        load(i + 1)
                    xi, xot = xin[i % 2], xo[i % 2]
                    for c in range(8):
                        bk = c % 8
                        for j in range(4):
                            op("pe", lambda q: q.transpose(ps[:, bk, j * 128:(j + 1) * 128],
                                                           xi[:, j, c * 128:(c + 1) * 128], ident_f[:]),
                               reads=[xi.b, ident_f.b], writes=[bank_b[bk]])
                        if c % 2 == 0:
                            op("act", lambda q: q.activation(out=xot[:, c, :], in_=ps[:, bk, :], func=AF.Copy),
                               reads=[bank_b[bk]], writes=[xot.b])
                        else:
                            op("dve", lambda q: q.tensor_copy(out=xot[:, c, :], in_=ps[:, bk, :]),
                               reads=[bank_b[bk]], writes=[xot.b])
                    T0 = i * 512
                    dma("sp", [(xv[:, :, T0:T0 + 512], xot[:])], reads=[xot.b], writes=[xT_b[0]])
                cx.barrier()

        def phase_transpose_out(cur):
            with ExitStack() as st:
                xin = [sb(st, "pf_x%d" % i, [128, 8, 512], F32) for i in range(2)]
                yo = [sb(st, "pf_o%d" % i, [128, 4, D], F32) for i in range(2)]
                ntile = TOK // 512
                xv = xT[cur].rearrange("(c p) t -> p c t", p=128)

                def load(i):
                    T0 = i * 512
                    dma("sp", [(xin[i % 2][:], xv[:, :, T0:T0 + 512])], reads=[xT_b[cur]], writes=[xin[i % 2].b])

                load(0)
                for i in range(ntile):
                    if i + 1 < ntile:
                        load(i + 1)
                    xi, yt = xin[i % 2], yo[i % 2]
                    for j in range(4):
                        for hh in range(2):
                            bk = (j * 2 + hh) % 8
                            for cc in range(4):
                                c = hh * 4 + cc
                                op("pe", lambda q: q.transpose(ps[:, bk, cc * 128:(cc + 1) * 128],
                                                               xi[:, c, j * 128:(j + 1) * 128], ident_f[:]),
                                   reads=[xi.b, ident_f.b], writes=[bank_b[bk]])
                            if hh == 0:
                                op("act", lambda q: q.activation(out=yt[:, j, hh * 512:(hh + 1) * 512],
                                                                 in_=ps[:, bk, :], func=AF.Copy),
                                   reads=[bank_b[bk]], writes=[yt.b])
                            else:
                                op("dve", lambda q: q.tensor_copy(out=yt[:, j, hh * 512:(hh + 1) * 512],
                                                                  in_=ps[:, bk, :]),
                                   reads=[bank_b[bk]], writes=[yt.b])
                    T0 = i * 512
                    dma("sp", [(y_all[T0:T0 + 512, :].rearrange("(j p) f -> p j f", p=128), yt[:])],
                        reads=[yt.b], writes=[Buf()])
                cx.barrier()

        def load_vec64(dst, col, src_row):
            s = src_row.rearrange("(p o) -> p o", o=1)
            dma("sp", [(dst[0:64, col:col + 1], s), (dst[64:128, col:col + 1], s)], writes=[dst.b])

        def rms_stats(src, n, sq, bank, lnv, rstd, eps_col, inv_n, nch=8):
            op("act", lambda q: q.activation(out=sq[:, 0:nch, 0:n], in_=src[:, 0:nch, 0:n], func=AF.Square),
               reads=[src.b], writes=[sq.b])
            for c in range(nch):
                op("pe", lambda q: q.matmul(ps[:, bank, 0:n], ones_m, sq[:, c, 0:n], start=(c == 0), stop=(c == nch - 1)),
                   reads=[sq.b, cmats.b], writes=[bank_b[bank]])
            op("act", lambda q: q.activation(out=lnv[:, 0:n], in_=ps[:, bank, 0:n], func=AF.Ln,
                                             bias=cst[:, eps_col:eps_col + 1], scale=inv_n),
               reads=[bank_b[bank], cst.b], writes=[lnv.b])
            op("act", lambda q: q.activation(out=rstd[:, 0:n], in_=lnv[:, 0:n], func=AF.Exp, scale=-0.5),
               reads=[lnv.b], writes=[rstd.b])

        def phase_in_proj(L, cur):
            even = (L % 2 == 0)
            jj = L // 2
            with ExitStack() as st, nc.allow_non_contiguous_dma(reason="small parameter vectors"):
                if even:
                    Wd, NF = win_ab_d[jj], 3840
                    fm_cols = [128 * i for i in range(6)] + [768 + 128 * i for i in range(6)] + \
                              [2304 + 128 * i for i in range(4)] + [2816 + 128 * i for i in range(4)]
                    fm_gain = [0] * 6 + [1] * 6 + [2] * 4 + [3] * 4
                    vpieces = [(1536, 512, 0), (2048, 256, 512), (3328, 512, 768)]
                    nv = 1280
                    lt = 0
                else:
                    Wd, NF = win_c_d[jj], 1536
                    fm_cols = [128 * i for i in range(10)]
                    fm_gain = [0] * 8 + [1] * 2
                    vpieces = [(1280, 256, 0)]
                    nv = 256
                    lt = 1
                nfm = len(fm_cols)
                permM = cmats[:, 3 + lt, :]
                Wsb = sb(st, "p1_w", [128, 8, NF], BF16)
                dma("sp", [(Wsb[:], Wd.rearrange("(k p) n -> p k n", p=128))], reads=[wts_b], writes=[Wsb.b])
                gv = sb(st, "p1_g", [128, 4], F32)
                if even:
                    for col, src in enumerate((qn_a, kn_a, qn_b, kn_b)):
                        load_vec64(gv, col, src[jj])
                else:
                    for col, src in enumerate((qn_c, kn_c)):
                        load_vec64(gv, col, src[jj])
                gmix = sb(st, "p1_gm", [128, 8], F32)
                dma("sp", [(gmix[:], norm_mix[L].rearrange("(c p) -> p c", p=128))], writes=[gmix.b])
                xt = [sb(st, "p1_x%d" % i, [128, 8, 512], F32) for i in range(3)]
                cs = [sb(st, "p1_cs%d" % i, [128, 2, 512], F32) for i in range(3)]
                sq = sb(st, "p1_sq", [128, 8, 512], BF16)
                hT = [sb(st, "p1_h%d" % i, [128, 8, 512], BF16) for i in range(2)]
                lnv = sb(st, "p1_ln", [128, 512], F32)
                rstd = sb(st, "p1_rs", [128, 512], F32)
                sqc = [sb(st, "p1_sqc%d" % i, [128, 512], BF16) for i in range(2)]
                lnq = [sb(st, "p1_lnq%d" % i, [128, 512], F32) for i in range(2)]
                rsq = [sb(st, "p1_rsq%d" % i, [128, 512], F32) for i in range(2)]
                av = [sb(st, "p1_a%d" % i, [128, 512], BF16) for i in range(2)]
                t1 = [sb(st, "p1_t1%d" % i, [128, 512], F32) for i in range(2)]
                t2 = [sb(st, "p1_t2%d" % i, [128, 512], F32) for i in range(2)]
                qkr = [sb(st, "p1_qk%d" % i, [128, 512], BF16) for i in range(4)]
                vr = [sb(st, "p1_v%d" % i, [128, nv], BF16) for i in range(2)]
                xv = xT[cur].rearrange("(c p) t -> p c t", p=128)
                qkv = qk_d.rearrange("(c p) t -> p c t", p=128)

                tiles = []
                for (s0, S) in seqs:
                    for t0 in range(0, S, 512):
                        tiles.append((s0 + t0, t0))
                nt = len(tiles)

                def load(i):
                    T0, t0 = tiles[i]
                    dma("sp", [(xt[i % 3][:], xv[:, :, T0:T0 + 512])], reads=[xT_b[cur]], writes=[xt[i % 3].b])
                    dma("sp", [(cs[i % 3][:], tabs_in[lt, :, :, t0:t0 + 512].rearrange("a p t -> p a t"))],
                        writes=[cs[i % 3].b])

                def norm_stage(i):
                    x_, h_ = xt[i % 3], hT[i % 2]
                    rms_stats(x_, 512, sq, 7, lnv, rstd, 0, 1.0 / D)
                    for c in range(8):
                        op("dve", lambda q: q.scalar_tensor_tensor(out=h_[:, c, :], in0=x_[:, c, :],
                                                                   scalar=gmix[:, c:c + 1], in1=rstd[:],
                                                                   op0=ALU.mult, op1=ALU.mult),
                           reads=[x_.b, gmix.b, rstd.b], writes=[h_.b])

                vjobs = [(s_, p) for s_ in range(4) for p in vpieces]

                def S1(i, c, j):
                    h_ = hT[i % 2]
                    bk = j % 3
                    wc = fm_cols[c]
                    for k in range(8):
                        op("pe", lambda q: q.matmul(ps[:, bk, :], Wsb[:, k, wc:wc + 128], h_[:, k, :],
                                                    start=(k == 0), stop=(k == 7)),
                           reads=[Wsb.b, h_.b], writes=[bank_b[bk]])
                    op("act", lambda q: q.activation(out=sqc[j % 2][:], in_=ps[:, bk, :], func=AF.Square),
                       reads=[bank_b[bk]], writes=[sqc[j % 2].b])

                def S2(i, c, j):
                    bk, b2 = j % 3, 3 + j % 2
                    op("pe", lambda q: q.matmul(ps[:, b2, :], blk_m, sqc[j % 2][:], start=True, stop=True),
                       reads=[sqc[j % 2].b, cmats.b], writes=[bank_b[b2]])
                    op("act", lambda q: q.activation(out=lnq[j % 2][:], in_=ps[:, b2, :], func=AF.Ln,
                                                     bias=cst[:, 0:1], scale=1.0 / 64),
                       reads=[bank_b[b2], cst.b], writes=[lnq[j % 2].b])
                    op("act", lambda q: q.activation(out=rsq[j % 2][:], in_=lnq[j % 2][:], func=AF.Exp, scale=-0.5),
                       reads=[lnq[j % 2].b], writes=[rsq[j % 2].b])
                    gi = fm_gain[c]
                    op("dve", lambda q: q.scalar_tensor_tensor(out=av[j % 2][:], in0=ps[:, bk, :],
                                                               scalar=gv[:, gi:gi + 1], in1=rsq[j % 2][:],
                                                               op0=ALU.mult, op1=ALU.mult),
                       reads=[bank_b[bk], gv.b, rsq[j % 2].b], writes=[av[j % 2].b])

                def S3(i, c, j):
                    T0, t0 = tiles[i]
                    cs_ = cs[i % 3]
                    b3 = 5 + j % 2
                    op("pe", lambda q: q.matmul(ps[:, b3, :], permM, av[j % 2][:], start=True, stop=True),
                       reads=[av[j % 2].b, cmats.b], writes=[bank_b[b3]])
                    op("dve", lambda q: q.tensor_tensor(out=t1[j % 2][:], in0=av[j % 2][:], in1=cs_[:, 0, :], op=ALU.mult),
                       reads=[av[j % 2].b, cs_.b], writes=[t1[j % 2].b])
                    op("dve", lambda q: q.tensor_tensor(out=t2[j % 2][:], in0=ps[:, b3, :], in1=cs_[:, 1, :], op=ALU.mult),
                       reads=[bank_b[b3], cs_.b], writes=[t2[j % 2].b])
                    qo = qkr[j % 4]
                    op("dve", lambda q: q.tensor_tensor(out=qo[:], in0=t1[j % 2][:], in1=t2[j % 2][:], op=ALU.add),
                       reads=[t1[j % 2].b, t2[j % 2].b], writes=[qo.b])
                    dma("sp", [(qk_d[c * 128:(c + 1) * 128, T0:T0 + 512], qo[:])], reads=[qo.b], writes=[qk_b])

                def VJ(i, idx):
                    T0, t0 = tiles[i]
                    h_ = hT[i % 2]
                    s_, (wc, n, dc) = vjobs[idx]
                    for k in range(8):
                        op("pe", lambda q: q.matmul(ps[:, 7, 0:n], h_[:, k, s_ * 128:(s_ + 1) * 128], Wsb[:, k, wc:wc + n],
                                                    start=(k == 0), stop=(k == 7)),
                           reads=[Wsb.b, h_.b], writes=[bank_b[7]])
                    vo = vr[s_ % 2]
                    op("act", lambda q: q.activation(out=vo[:, dc:dc + n], in_=ps[:, 7, 0:n], func=AF.Copy),
                       reads=[bank_b[7]], writes=[vo.b])
                    if idx % len(vpieces) == len(vpieces) - 1:
                        dma("sp", [(v_d[T0 + s_ * 128:T0 + (s_ + 1) * 128, 0:nv], vo[:])], reads=[vo.b], writes=[v_b])

                jobs = [(i, c) for i in range(nt) for c in range(nfm)]
                nj = len(jobs)
                assert len(vjobs) <= nfm
                load(0)
                if nt > 1:
                    load(1)
                norm_stage(0)
                for s_i in range(nj + 2):
                    if s_i < nj:
                        i, c = jobs[s_i]
                        if c == 2 and i + 2 < nt:
                            load(i + 2)
                        if c == nfm // 2 and i + 1 < nt:
                            norm_stage(i + 1)
                        S1(i, c, s_i)
                    if 0 <= s_i - 1 < nj:
                        S2(jobs[s_i - 1][0], jobs[s_i - 1][1], s_i - 1)
                    if 0 <= s_i - 2 < nj:
                        S3(jobs[s_i - 2][0], jobs[s_i - 2][1], s_i - 2)
                    if s_i < nj and c < len(vjobs):
                        VJ(i, c)
                cx.barrier()

        class Unit:
            __slots__ = ("pre", "qk", "nb", "mask", "pv", "fin", "rd", "den")

            def __init__(self):
                self.den = None

        def run_units(units, P, Psm, masks_t, bg=None, bg_every=8, lazy=None, ng=2, lag=1):
            n = len(units)
            ident_b = cmats[:, 0, :]
            pending = [None]

            def qk_exp(u):
                un = units[u]
                g0 = (u % ng) * 2
                for e, (lhsT, rhs) in enumerate(un.qk):
                    if un.mask is None:
                        op("pe", lambda q: q.matmul(ps[:, g0 + e, :], lhsT, rhs, start=True, stop=True),
                           reads=un.rd, writes=[bank_b[g0 + e]])
                    else:
                        mi = un.mask + e
                        op("pe", lambda q: q.matmul(ps[:, g0 + e, :], lhsT, rhs, start=True, stop=False),
                           reads=un.rd, writes=[bank_b[g0 + e]])
                        op("pe", lambda q: q.matmul(ps[:, g0 + e, :], ident_b, masks_t[:, mi, :], start=False, stop=True),
                           reads=[cmats.b, masks_t.b], writes=[bank_b[g0 + e]])
                nb = un.nb
                Pt = P[u % len(P)]
                op("act", lambda q: q.activation(out=Pt[:, 0:nb, :], in_=ps[:, g0:g0 + nb, :], func=AF.Exp, scale=SCALE),
                   reads=[bank_b[g0 + e_] for e_ in range(nb)], writes=[Pt.b])
                if un.den is not None:
                    Pq = Psm[u % 3]
                    op("dve", lambda q: q.tensor_tensor(out=Pq[:], in0=Pt[:, 0, :], in1=Pt[:, 1, :], op=ALU.add),
                       reads=[Pt.b], writes=[Pq.b])
                if lazy:
                    lazy.pop(0)()

            def den_mm(u):
                un = units[u]
                bank, start, stop = un.den
                Pq = Psm[u % 3]
                op("pe", lambda q: q.matmul(ps[:, bank, :], ones_m, Pq[:], start=start, stop=stop),
                   reads=[Pq.b, cmats.b], writes=[bank_b[bank]])

            def pv(u):
                un = units[u]
                Pt = P[u % len(P)]
                if pending[0] is not None:
                    den_mm(pending[0])
                    pending[0] = None
                for (bank, lhsT, e, start, stop, rd) in un.pv:
                    op("pe", lambda q: q.matmul(ps[:, bank, :], lhsT, Pt[:, e, :], start=start, stop=stop),
                       reads=[Pt.b] + rd, writes=[bank_b[bank]])
                if un.den is not None:
                    if un.fin is not None:
                        den_mm(u)
                    else:
                        pending[0] = u
                if un.fin is not None:
                    un.fin()

            for u in range(n + lag):
                if u < n:
                    qk_exp(u)
                if 0 <= u - lag < n:
                    pv(u - lag)
                w_ = u - lag + 1
                if 0 <= w_ < n and units[w_].pre is not None:
                    units[w_].pre()
                if bg and u % bg_every == bg_every - 1:
                    bg.pop(0)()
            while bg:
                bg.pop(0)()
            while lazy:
                lazy.pop(0)()

        def fin_pair(accA, accB, rec, ot, dst_ap, dst_b):
            op("dve", lambda q: q.reciprocal(out=rec[0:64, :], in_=ps[64:128, accA, :]),
               reads=[bank_b[accA]], writes=[rec.b])
            op("dve", lambda q: q.tensor_tensor(out=ot[0:64, :], in0=ps[0:64, accA, :], in1=rec[0:64, :], op=ALU.mult),
               reads=[bank_b[accA], rec.b], writes=[ot.b])
            op("dve", lambda q: q.reciprocal(out=rec[64:128, :], in_=ps[0:64, accB, :]),
               reads=[bank_b[accB]], writes=[rec.b])
            op("dve", lambda q: q.tensor_tensor(out=ot[64:128, :], in0=ps[64:128, accB, :], in1=rec[64:128, :], op=ALU.mult),
               reads=[bank_b[accB], rec.b], writes=[ot.b])
            dma("sp", [(dst_ap, ot[:])], reads=[ot.b], writes=[dst_b])

        def phase_attn_c(L):
            with ExitStack() as st:
                KT = [[sb(st, "c_k%d_%d" % (i, v), [128, TABLE_S], BF16) for v in range(2)] for i in range(2)]
                for kk in KT:
                    for k_ in kk:
                        op("dve", lambda q: q.memset(k_[:], 0.0), writes=[k_.b])
                VE = [sb(st, "c_v%d" % i, [128, TABLE_S // 128, 192], BF16) for i in range(2)]
                QT = [sb(st, "c_q%d" % i, [128, TABLE_S], BF16) for i in range(2)]
                P = [sb(st, "c_p%d" % i, [128, 2, 512], BF16) for i in range(4)]
                cacc = [[sb(st, "c_ca%d_%d" % (i, v), [128, 512], F32) for v in range(2)] for i in range(2)]
                rec = [sb(st, "c_r%d" % i, [128, 512], F32) for i in range(2)]
                ot = [sb(st, "c_o%d" % i, [128, 512], BF16) for i in range(2)]
                for v in VE:
                    op("dve", lambda q: q.memset(v[:], 1.0), writes=[v.b])
                jobs = []
                for (s0, S) in seqs:
                    for n in range(4):
                        for i in range(2):
                            jobs.append((s0, S, n, i))

                def load_kv(jn):
                    s0, S, n, i = jobs[jn]
                    kv = jn // 2
                    src = qk_d[1024 + n * 64:1024 + (n + 1) * 64, s0:s0 + S]
                    dma("sp", [(KT[kv % 2][0][0:64, 0:S], src), (KT[kv % 2][1][64:128, 0:S], src)],
                        reads=[qk_b], writes=[KT[kv % 2][0].b, KT[kv % 2][1].b])
                    dma("sp", [(VE[kv % 2][:, 0:S // 128, 64:128],
                                v_d[s0:s0 + S, n * 64:(n + 1) * 64].rearrange("(c p) f -> p c f", p=128))],
                        reads=[v_b], writes=[VE[kv % 2].b])

                def load_q(jn):
                    s0, S, n, i = jobs[jn]
                    qc = 2 * n + i
                    dma("sp", [(QT[jn % 2][:, 0:S], qk_d[qc * 128:(qc + 1) * 128, s0:s0 + S])],
                        reads=[qk_b], writes=[QT[jn % 2].b])

                units = []
                fcount = [0]
                for jn, (s0, S, n, i) in enumerate(jobs):
                    kv = jn // 2
                    K_, V_, Q_ = KT[kv % 2], VE[kv % 2], QT[jn % 2]
                    qc = 2 * n + i
                    nkp = S // 256
                    for qt in range(S // 512):
                        par = fcount[0] % 2
                        fcount[0] += 1
                        acc = (6, 7)
                        for kp in range(nkp):
                            for hb in range(2):
                                un = Unit()
                                un.pre = None
                                if qt == 0 and kp == 0 and hb == 0:
                                    def pre(jn=jn):
                                        if jn + 1 < len(jobs):
                                            if (jn + 1) % 2 == 0:
                                                load_kv(jn + 1)
                                            load_q(jn + 1)
                                    un.pre = pre
                                un.qk = [(K_[hb][:, (2 * kp + e) * 128:(2 * kp + e + 1) * 128],
                                          Q_[:, qt * 512:(qt + 1) * 512]) for e in range(2)]
                                un.nb = 2
                                un.mask = None
                                un.rd = [K_[hb].b, Q_.b]
                                vs = slice(64, 192) if hb == 0 else slice(0, 128)
                                un.pv = [(acc[hb], V_[:, 2 * kp + e, vs], e, (kp == 0 and e == 0),
                                          (kp == nkp - 1 and e == 1), [V_.b]) for e in range(2)]
                                un.fin = None
                                if kp == nkp - 1 and hb == 1:
                                    def fin(acc=acc, par=par, qc=qc, s0=s0, qt=qt):
                                        cA, cB, rc_, o_ = cacc[par][0], cacc[par][1], rec[par], ot[par]
                                        op("dve", lambda q: q.tensor_copy(out=cA[:], in_=ps[:, 6, :]), reads=[bank_b[6]], writes=[cA.b])
                                        op("dve", lambda q: q.tensor_copy(out=cB[:], in_=ps[:, 7, :]), reads=[bank_b[7]], writes=[cB.b])
                                        op("dve", lambda q: q.reciprocal(out=rc_[0:64, :], in_=cA[64:128, :]), reads=[cA.b], writes=[rc_.b])
                                        op("dve", lambda q: q.tensor_tensor(out=o_[0:64, :], in0=cA[0:64, :], in1=rc_[0:64, :], op=ALU.mult),
                                           reads=[cA.b, rc_.b], writes=[o_.b])
                                        op("dve", lambda q: q.reciprocal(out=rc_[64:128, :], in_=cB[0:64, :]), reads=[cB.b], writes=[rc_.b])
                                        op("dve", lambda q: q.tensor_tensor(out=o_[64:128, :], in0=cB[64:128, :], in1=rc_[64:128, :], op=ALU.mult),
                                           reads=[cB.b, rc_.b], writes=[o_.b])
                                        dma("sp", [(at_d[qc * 128:(qc + 1) * 128, s0 + qt * 512:s0 + (qt + 1) * 512], o_[:])],
                                            reads=[o_.b], writes=[at_b])
                                    un.fin = fin
                                units.append(un)
                load_kv(0)
                load_q(0)
                run_units(units, P, None, None, ng=3, lag=2)
                cx.barrier()

        def phase_attn_ab(L):
            jj = L // 2
            lam_init = 0.8 - 0.6 * math.exp(-0.3 * L)
            with ExitStack() as st:
                masks_t = sb(st, "a_m", [128, NMASK, 512], BF16)
                dma("sp", [(masks_t[:, 0:17, :], masks_in[:, 0:17, :]), (masks_t[:, 17:34, :], masks_in[:, 17:34, :])],
                    writes=[masks_t.b])
                QW = [sb(st, "a_q%d" % i, [128, 3, 512], BF16) for i in range(2)]
                KW = [[sb(st, "a_k%d_%d" % (i, v), [128, NMASK * 128], BF16) for v in range(2)] for i in range(2)]
                for kk in KW:
                    for k_ in kk:
                        op("dve", lambda q: q.memset(k_[:], 0.0), writes=[k_.b])
                VW = [sb(st, "a_v%d" % i, [128, NMASK, 192], BF16) for i in range(2)]
                P = [sb(st, "a_p%d" % i, [128, 2, 512], BF16) for i in range(3)]
                rec = [sb(st, "a_r%d" % i, [128, 512], F32) for i in range(2)]
                ot = [sb(st, "a_o%d" % i, [128, 512], BF16) for i in range(2)]
                for v in VW:
                    op("dve", lambda q: q.memset(v[:], 1.0), writes=[v.b])
                jobs = []
                for (s0, S) in seqs:
                    for jp in range(2):
                        for qt in range(S // 512):
                            jobs.append((s0, S, jp, qt))

                def windows(S, qt):
                    res = []
                    slot = 0
                    for g, d in enumerate(A_DIL):
                        lo, hi = MASK_REL[g]
                        c0 = max(0, qt * 4 + lo)
                        c1 = min(S // 128 - 1, qt * 4 + hi)
                        res.append((g, c0, c1, slot))
                        slot += c1 - c0 + 1
                    return res

                def load_job(jn):
                    s0, S, jp, qt = jobs[jn]
                    Qb, Kb, Vb = QW[jn % 2], KW[jn % 2], VW[jn % 2]
                    xq, xk, xv_ = [], [], []
                    for (g, c0, c1, slot) in windows(S, qt):
                        ch = 2 * g + jp
                        xq.append((Qb[:, g, :], qk_d[ch * 128:(ch + 1) * 128, s0 + qt * 512:s0 + (qt + 1) * 512]))
                        nck = c1 - c0 + 1
                        xk.append((Kb[0][0:64, slot * 128:(slot + nck) * 128],
                                   qk_d[768 + ch * 128:768 + ch * 128 + 64, s0 + c0 * 128:s0 + (c1 + 1) * 128]))
                        xk.append((Kb[1][64:128, slot * 128:(slot + nck) * 128],
                                   qk_d[768 + ch * 128 + 64:768 + (ch + 1) * 128, s0 + c0 * 128:s0 + (c1 + 1) * 128]))
                        vsrc = v_d[s0 + c0 * 128:s0 + (c1 + 1) * 128, :]
                        f0 = g * 256 + jp * 128
                        xv_.append((Vb[:, slot:slot + nck, 0:64], vsrc[:, f0:f0 + 64].rearrange("(c p) f -> p c f", p=128)))
                        xv_.append((Vb[:, slot:slot + nck, 128:192], vsrc[:, f0 + 64:f0 + 128].rearrange("(c p) f -> p c f", p=128)))
                    dma("sp", xq, reads=[qk_b], writes=[Qb.b])
                    dma("sp", xk, reads=[qk_b], writes=[Kb[0].b, Kb[1].b])
                    dma("sp", xv_, reads=[v_b], writes=[Vb.b])

                units = []
                for jn, (s0, S, jp, qt) in enumerate(jobs):
                    Qb, Kb, Vb = QW[jn % 2], KW[jn % 2], VW[jn % 2]
                    par = jn % 2
                    acc = (4 + 2 * par, 5 + 2 * par)
                    wins = windows(S, qt)
                    ulist = []
                    for hb in range(2):
                        for (g, c0, c1, slot) in wins:
                            c = c0
                            while c <= c1:
                                nb = 2 if c + 1 <= c1 else 1
                                ulist.append((hb, g, c, nb, slot + (c - c0)))
                                c += nb
                    first = {0: True, 1: True}
                    lastidx = {}
                    for ui, (hb, g, c, nb, sl) in enumerate(ulist):
                        lastidx[hb] = ui
                    for ui, (hb, g, c, nb, sl) in enumerate(ulist):
                        un = Unit()
                        un.pre = None
                        if ui == 0:
                            def pre(jn=jn):
                                if jn + 1 < len(jobs):
                                    load_job(jn + 1)
                            un.pre = pre
                        un.qk = [(Kb[hb][:, (sl + e) * 128:(sl + e + 1) * 128], Qb[:, g, :]) for e in range(nb)]
                        un.nb = nb
                        lo = MASK_REL[g][0]
                        un.mask = MASK_BASE[g] + (c - qt * 4) - lo
                        un.rd = [Kb[hb].b, Qb.b]
                        vs = slice(0, 128) if hb == 0 else slice(64, 192)
                        un.pv = []
                        for e in range(nb):
                            un.pv.append((acc[hb], Vb[:, sl + e, vs], e, first[hb], (ui == lastidx[hb] and e == nb - 1), [Vb.b]))
                            first[hb] = False
                        un.fin = None
                        if ui == len(ulist) - 1:
                            def fin(acc=acc, par=par, jp=jp, s0=s0, qt=qt):
                                fin_pair(acc[0], acc[1], rec[par], ot[par],
                                         at_d[jp * 128:(jp + 1) * 128, s0 + qt * 512:s0 + (qt + 1) * 512], at_b)
                            un.fin = fin
                        units.append(un)
                bg = []
                if deferred:
                    dci = [sb(st, "dc_i%d" % i, [128, 2048], F32) for i in range(2)]
                    dco = [sb(st, "dc_o%d" % i, [128, 2048], BF16) for i in range(2)]

                    def mk(k, d2, s2, w):
                        def job():
                            ci, co = dci[k % 2], dco[k % 2]
                            dma("sp", [(ci[:, 0:w], s2)], writes=[ci.b])
                            op("dve", lambda q: q.tensor_copy(out=co[:, 0:w], in_=ci[:, 0:w]), reads=[ci.b], writes=[co.b])
                            dma("sp", [(d2, co[:, 0:w])], reads=[co.b], writes=[wts_b])
                        return job
                    for k, (d2, s2, w) in enumerate(deferred):
                        bg.append(mk(k, d2, s2, w))
                    for L_ in deferred_rearr:
                        bg.append(lambda L_=L_: rearr_wup(L_))
                    del deferred[:]
                    del deferred_rearr[:]
                load_job(0)
                run_units(units, P, None, masks_t, bg=bg, bg_every=max(1, (len(units) - 8) // max(1, len(bg))))
                cx.barrier()

            with ExitStack() as st, nc.allow_non_contiguous_dma(reason="small parameter vectors"):
                lv = sb(st, "b_lv", [128, 4, 64], F32)
                for col, src in enumerate((lq1, lk1, lq2, lk2)):
                    dma("sp", [(lv[:, col, :], src[jj].partition_broadcast(128))], writes=[lv.b])
                lp = sb(st, "b_lp", [128, 2, 64], F32)
                ls = sb(st, "b_ls", [128, 4], F32)
                op("dve", lambda q: q.tensor_tensor(out=lp[:, 0, :], in0=lv[:, 0, :], in1=lv[:, 1, :], op=ALU.mult),
                   reads=[lv.b], writes=[lp.b])
                op("dve", lambda q: q.tensor_tensor(out=lp[:, 1, :], in0=lv[:, 2, :], in1=lv[:, 3, :], op=ALU.mult),
                   reads=[lv.b], writes=[lp.b])
                op("dve", lambda q: q.tensor_reduce(out=ls[:, 0:2], in_=lp[:], axis=mybir.AxisListType.X, op=ALU.add),
                   reads=[lp.b], writes=[ls.b])
                op("act", lambda q: q.activation(out=ls[:, 2:4], in_=ls[:, 0:2], func=AF.Exp), reads=[ls.b], writes=[ls.b])
                nlam = sb(st, "b_nl", [128, 1], F32)
                op("dve", lambda q: q.scalar_tensor_tensor(out=nlam[:], in0=ls[:, 3:4], scalar=-lam_init, in1=ls[:, 2:3],
                                                           op0=ALU.add, op1=ALU.subtract),
                   reads=[ls.b], writes=[nlam.b])
                gs = sb(st, "b_gs", [128, 1], F32)
                dma("sp", [(gs[:], subln[jj].rearrange("(p o) -> p o", o=1))], writes=[gs.b])
                op("dve", lambda q: q.tensor_scalar(out=gs[:], in0=gs[:], scalar1=1.0 - lam_init, scalar2=None, op0=ALU.mult),
                   reads=[gs.b], writes=[gs.b])

                KT = [[sb(st, "b_k%d_%d" % (i, v), [128, TABLE_S], BF16) for v in range(2)] for i in range(2)]
                for kk in KT:
                    for k_ in kk:
                        op("dve", lambda q: q.memset(k_[:], 0.0), writes=[k_.b])
                VV = [sb(st, "b_v%d" % i, [128, TABLE_S // 128, 128], BF16) for i in range(2)]
                QT = [sb(st, "b_q%d" % i, [128, TABLE_S], BF16) for i in range(2)]
                P = [sb(st, "b_p%d" % i, [128, 2, 512], BF16) for i in range(3)]
                Psm = [sb(st, "b_ps%d" % i, [128, 512], BF16) for i in range(3)]
                r0t = sb(st, "b_r0", [128, 512], F32)
                r1t = sb(st, "b_r1", [128, 512], F32)
                u0t = sb(st, "b_u0", [128, 512], F32)
                u1t = sb(st, "b_u1", [128, 512], F32)
                ob = sb(st, "b_ob", [128, 512], F32)
                osq = sb(st, "b_sq", [128, 512], BF16)
                oln = sb(st, "b_ln", [128, 512], F32)
                ors = sb(st, "b_rs", [128, 512], F32)
                ot = [sb(st, "b_o%d" % i, [128, 512], BF16) for i in range(2)]
                jobs = []
                for (s0, S) in seqs:
                    for h in range(4):
                        jobs.append((s0, S, h))

                def load_job(jn):
                    s0, S, h = jobs[jn]
                    dma("sp", [(KT[jn % 2][0][0:64, 0:S], qk_d[2048 + h * 128:2048 + h * 128 + 64, s0:s0 + S]),
                               (KT[jn % 2][1][64:128, 0:S], qk_d[2048 + h * 128 + 64:2048 + (h + 1) * 128, s0:s0 + S])],
                        reads=[qk_b], writes=[KT[jn % 2][0].b, KT[jn % 2][1].b])
                    dma("sp", [(QT[jn % 2][:, 0:S], qk_d[1536 + h * 128:1536 + (h + 1) * 128, s0:s0 + S])],
                        reads=[qk_b], writes=[QT[jn % 2].b])
                    dma("sp", [(VV[jn % 2][:, 0:S // 128, :],
                                v_d[s0:s0 + S, 768 + h * 128:768 + (h + 1) * 128].rearrange("(c p) f -> p c f", p=128))],
                        reads=[v_b], writes=[VV[jn % 2].b])

                units = []
                lazy = []
                fc = [0]
                for jn, (s0, S, h) in enumerate(jobs):
                    K_, V_, Q_ = KT[jn % 2], VV[jn % 2], QT[jn % 2]
                    nkp = S // 256
                    for qt in range(S // 512):
                        par = fc[0] % 2
                        fc[0] += 1
                        for kp in range(nkp):
                            for m in range(2):
                                un = Unit()
                                un.pre = None
                                if qt == 0 and kp == 0 and m == 0:
                                    def pre(jn=jn):
                                        if jn + 1 < len(jobs):
                                            load_job(jn + 1)
                                    un.pre = pre
                                un.qk = [(K_[m][:, (2 * kp + e) * 128:(2 * kp + e + 1) * 128],
                                          Q_[:, qt * 512:(qt + 1) * 512]) for e in range(2)]
                                un.nb = 2
                                un.mask = None
                                un.rd = [K_[m].b, Q_.b]
                                un.pv = []
                                for e in range(2):
                                    st_ = (kp == 0 and e == 0)
                                    sp_ = (kp == nkp - 1 and e == 1)
                                    un.pv.append((4 + 2 * m, V_[:, 2 * kp + e, :], e, st_, sp_, [V_.b]))
                                un.den = (5 + 2 * m, kp == 0, kp == nkp - 1)
                                un.fin = None
                                if kp == nkp - 1 and m == 1:
                                    def fin(par=par, h=h, s0=s0, qt=qt):
                                        while lazy:
                                            lazy.pop(0)()
                                        o_ = ot[par]
                                        for bk_, dst_ in ((4, u0t), (5, r0t), (6, u1t), (7, r1t)):
                                            op("dve", lambda q: q.tensor_copy(out=dst_[:], in_=ps[:, bk_, :]),
                                               reads=[bank_b[bk_]], writes=[dst_.b])

                                        def rc(t_, lo_, hi_):
                                            return lambda: op("dve", lambda q: q.reciprocal(out=t_[:, lo_:hi_], in_=t_[:, lo_:hi_]),
                                                              reads=[t_.b], writes=[t_.b])

                                        def ml(u_, r_):
                                            return lambda: op("dve", lambda q: q.tensor_tensor(out=u_[:], in0=u_[:], in1=r_[:], op=ALU.mult),
                                                              reads=[u_.b, r_.b], writes=[u_.b])

                                        def mid():
                                            op("dve", lambda q: q.scalar_tensor_tensor(out=ob[:], in0=u1t[:], scalar=nlam[:, 0:1], in1=u0t[:],
                                                                                       op0=ALU.mult, op1=ALU.add),
                                               reads=[u1t.b, u0t.b, nlam.b], writes=[ob.b])
                                            op("act", lambda q: q.activation(out=osq[:], in_=ob[:], func=AF.Square), reads=[ob.b], writes=[osq.b])
                                            op("pe", lambda q: q.matmul(ps[:, 0, :], ones_m, osq[:], start=True, stop=True),
                                               reads=[osq.b, cmats.b], writes=[bank_b[0]])
                                            op("act", lambda q: q.activation(out=oln[:], in_=ps[:, 0, :], func=AF.Ln, bias=cst[:, 1:2], scale=1.0 / 128),
                                               reads=[bank_b[0], cst.b], writes=[oln.b])
                                            op("act", lambda q: q.activation(out=ors[:], in_=oln[:], func=AF.Exp, scale=-0.5), reads=[oln.b], writes=[ors.b])

                                        def last(o_=o_, h=h, s0=s0, qt=qt):
                                            op("dve", lambda q: q.scalar_tensor_tensor(out=o_[:], in0=ob[:], scalar=gs[:, 0:1], in1=ors[:],
                                                                                       op0=ALU.mult, op1=ALU.mult),
                                               reads=[ob.b, gs.b, ors.b], writes=[o_.b])
                                            dma("sp", [(at_d[256 + h * 128:256 + (h + 1) * 128, s0 + qt * 512:s0 + (qt + 1) * 512], o_[:])],
                                                reads=[o_.b], writes=[at_b])

                                        lazy.extend([rc(r0t, 0, 256), rc(r0t, 256, 512), ml(u0t, r0t),
                                                     rc(r1t, 0, 256), rc(r1t, 256, 512), ml(u1t, r1t), mid, last])
                                    un.fin = fin
                                units.append(un)
                load_job(0)
                run_units(units, P, Psm, None, lazy=lazy)
                cx.barrier()

        def phase_ffn(L, cur):
            even = (L % 2 == 0)
            jj = L // 2
            nfc = 6 if even else 8
            Wod = wout_ab_d[jj] if even else wout_c_d[jj]
            with ExitStack() as st, nc.allow_non_contiguous_dma(reason="small parameter vectors"):
                Wo = sb(st, "f_wo", [128, nfc, D], BF16)
                dma("sp", [(Wo[:], Wod.rearrange("(k p) n -> p k n", p=128))], reads=[wts_b], writes=[Wo.b])
                Wdn = sb(st, "f_wd", [128, NPAIR, D], BF16)
                dma("sp", [(Wdn[:, 0:11, :], wdown_d[L, 0:11 * 128, :].rearrange("(k p) n -> p k n", p=128)),
                           (Wdn[:, 11:22, :], wdown_d[L, 11 * 128:22 * 128, :].rearrange("(k p) n -> p k n", p=128))],
                    reads=[wts_b], writes=[Wdn.b])
                gf = sb(st, "f_g", [128, 8], F32)
                dma("sp", [(gf[:], norm_ffn[L].rearrange("(c p) -> p c", p=128))], writes=[gf.b])
                cw = sb(st, "f_cw", [128, 3, 2 * NPAIR], F32)
                dma("sp", [(cw[:, t, :], conv_w[L, t].rearrange("(m p) -> p m", p=128)) for t in range(3)], writes=[cw.b])
                cb = sb(st, "f_cb", [128, 2 * NPAIR], F32)
                dma("sp", [(cb[:], conv_b[L].rearrange("(m p) -> p m", p=128))], writes=[cb.b])
                NW = 3
                Wu = [sb(st, "f_wu%d" % i, [128, 8, 2, 128], BF16) for i in range(NW)]
                xt = [sb(st, "f_x%d" % i, [128, 8, 512], F32) for i in range(2)]
                att = [sb(st, "f_a%d" % i, [128, nfc, 512], BF16) for i in range(2)]
                x1 = sb(st, "f_x1", [128, 8, 512], F32)
                x2 = sb(st, "f_x2", [128, 8, 512], F32)
                sq = sb(st, "f_sq", [128, 8, 512], BF16)
                hT = sb(st, "f_h", [128, 8, 512], BF16)
                lnv = sb(st, "f_ln", [128, 512], F32)
                rstd = sb(st, "f_rs", [128, 512], F32)
                gT = sb(st, "f_gT", [128, NPAIR, 512], BF16)
                ca = [sb(st, "f_ca%d" % i, [128, 512], F32) for i in range(2)]
                cbv = [sb(st, "f_cv%d" % i, [128, 512], F32) for i in range(2)]
                xr = xT[cur].rearrange("(c p) t -> p c t", p=128)
                xw = xT[1 - cur].rearrange("(c p) t -> p c t", p=128)
                atv = at_d.rearrange("(c p) t -> p c t", p=128)

                tiles = []
                for (s0, S) in seqs:
                    o0 = 0
                    while o0 < S:
                        o1 = min(S, o0 + 496)
                        lo = max(0, o0 - 1)
                        hi = min(S, o1 + 1)
                        tiles.append((s0, S, lo, hi, o0, o1))
                        o0 = o1
                nt = len(tiles)
                wu_ctr = [0]

                def load_x(i):
                    s0, S, lo, hi, o0, o1 = tiles[i]
                    n = hi - lo
                    dma("sp", [(xt[i % 2][:, :, 0:n], xr[:, :, s0 + lo:s0 + hi])], reads=[xT_b[cur]], writes=[xt[i % 2].b])
                    dma("sp", [(att[i % 2][:, :, 0:n], atv[:, 0:nfc, s0 + lo:s0 + hi])], reads=[at_b], writes=[att[i % 2].b])

                def load_wu(idx):
                    m = idx % NPAIR
                    w_ = Wu[idx % NW]
                    dma("sp", [(w_[:], wup_p[L, m])], reads=[wupp_b], writes=[w_.b])

                total_pieces = nt * NPAIR
                for idx in range(min(NW - 1, total_pieces)):
                    load_wu(idx)
                load_x(0)
                for i in range(nt):
                    s0, S, lo, hi, o0, o1 = tiles[i]
                    n = hi - lo
                    if i + 1 < nt:
                        load_x(i + 1)
                    x_, a_ = xt[i % 2], att[i % 2]
                    for c in range(8):
                        bk = 4 + c % 2
                        for k in range(nfc):
                            op("pe", lambda q: q.matmul(ps[:, bk, 0:n], Wo[:, k, c * 128:(c + 1) * 128], a_[:, k, 0:n],
                                                        start=(k == 0), stop=(k == nfc - 1)),
                               reads=[Wo.b, a_.b], writes=[bank_b[bk]])
                        op("dve", lambda q: q.tensor_tensor(out=x1[:, c, 0:n], in0=ps[:, bk, 0:n], in1=x_[:, c, 0:n], op=ALU.add),
                           reads=[bank_b[bk], x_.b], writes=[x1.b])
                    rms_stats(x1, n, sq, 6, lnv, rstd, 0, 1.0 / D)
                    for c in range(8):
                        op("dve", lambda q: q.scalar_tensor_tensor(out=hT[:, c, 0:n], in0=x1[:, c, 0:n], scalar=gf[:, c:c + 1],
                                                                   in1=rstd[:, 0:n], op0=ALU.mult, op1=ALU.mult),
                           reads=[x1.b, gf.b, rstd.b], writes=[hT.b])
                    left_pad = (lo == 0)
                    right_pad = (hi == S)

                    def conv(bank, dst, m, half):
                        col = half * NPAIR + m
                        op("act", lambda q: q.activation(out=dst[:, 0:n], in_=ps[:, bank, 0:n], func=AF.Identity,
                                                         bias=cb[:, col:col + 1], scale=cw[:, 1, col:col + 1]),
                           reads=[bank_b[bank], cb.b, cw.b], writes=[dst.b])
                        op("dve", lambda q: q.scalar_tensor_tensor(out=dst[:, 1:n], in0=ps[:, bank, 0:n - 1],
                                                                   scalar=cw[:, 0, col:col + 1], in1=dst[:, 1:n],
                                                                   op0=ALU.mult, op1=ALU.add),
                           reads=[bank_b[bank], cw.b, dst.b], writes=[dst.b])
                        op("dve", lambda q: q.scalar_tensor_tensor(out=dst[:, 0:n - 1], in0=ps[:, bank, 1:n],
                                                                   scalar=cw[:, 2, col:col + 1], in1=dst[:, 0:n - 1],
                                                                   op0=ALU.mult, op1=ALU.add),
                           reads=[bank_b[bank], cw.b, dst.b], writes=[dst.b])

                    def U1(m):
                        idx = wu_ctr[0]
                        wu_ctr[0] += 1
                        if idx + NW - 1 < total_pieces:
                            load_wu(idx + NW - 1)
                        w_ = Wu[idx % NW]
                        for t in range(2):
                            bk = (m % 2) * 2 + t
                            for k in range(8):
                                op("pe", lambda q: q.matmul(ps[:, bk, 0:n], w_[:, k, t, :], hT[:, k, 0:n],
                                                            start=(k == 0), stop=(k == 7)),
                                   reads=[w_.b, hT.b], writes=[bank_b[bk]])

                    def U2(m):
                        b0 = (m % 2) * 2
                        conv(b0, ca[m % 2], m, 0)
                        conv(b0 + 1, cbv[m % 2], m, 1)
                        op("act", lambda q: q.activation(out=ca[m % 2][:, 0:n], in_=ca[m % 2][:, 0:n], func=AF.Silu),
                           reads=[ca[m % 2].b], writes=[ca[m % 2].b])
                        op("dve", lambda q: q.tensor_tensor(out=gT[:, m, 0:n], in0=ca[m % 2][:, 0:n], in1=cbv[m % 2][:, 0:n], op=ALU.mult),
                           reads=[ca[m % 2].b, cbv[m % 2].b], writes=[gT.b])

                    for m in range(NPAIR + 1):
                        if m < NPAIR:
                            U1(m)
                        if m >= 1:
                            U2(m - 1)
                    for c in range(8):
                        bk = 4 + c % 2
                        for m in range(NPAIR):
                            op("pe", lambda q: q.matmul(ps[:, bk, 0:n], Wdn[:, m, c * 128:(c + 1) * 128], gT[:, m, 0:n],
                                                        start=(m == 0), stop=(m == NPAIR - 1)),
                               reads=[Wdn.b, gT.b], writes=[bank_b[bk]])
                        op("dve", lambda q: q.tensor_tensor(out=x2[:, c, 0:n], in0=ps[:, bk, 0:n], in1=x1[:, c, 0:n], op=ALU.add),
                           reads=[bank_b[bk], x1.b], writes=[x2.b])
                    a0 = o0 - lo
                    a1 = o1 - lo
                    dma("sp", [(xw[:, :, s0 + o0:s0 + o1], x2[:, :, a0:a1])], reads=[x2.b], writes=[xT_b[1 - cur]])
                cx.barrier()

        phase_transpose_in()
        cur = 0
        for L in range(depth):
            phase_in_proj(L, cur)
            if L % 2 == 0:
                phase_attn_ab(L)
            else:
                phase_attn_c(L)
            phase_ffn(L, cur)
            cur = 1 - cur
        phase_transpose_out(cur)
        if debug:
            dq = nc.dram_tensor("dbg_qk", [2560, TOK], BF16, kind="ExternalOutput").ap()
            dv = nc.dram_tensor("dbg_v", [TOK, 1280], BF16, kind="ExternalOutput").ap()
            da = nc.dram_tensor("dbg_at", [D, TOK], BF16, kind="ExternalOutput").ap()
            dma("sp", [(dq[:, :], qk_d[:, :]), (dv[:, :], v_d[:, :]), (da[:, :], at_d[:, :])], reads=[qk_b, v_b, at_b], writes=[Buf()])
            cx.barrier()
    return nc


_CONST_CACHE = {}


def _consts():
    if "c" not in _CONST_CACHE:
        _CONST_CACHE["c"] = host_constants()
    return _CONST_CACHE["c"]


WEIGHT_KEYS = ["norm_mix", "norm_ffn", "w_in_ab", "q_norm_a", "k_norm_a", "q_norm_b", "k_norm_b",
               "lambda_q1", "lambda_k1", "lambda_q2", "lambda_k2", "subln_b", "w_out_ab",
               "w_in_c", "q_norm_c", "k_norm_c", "w_out_c", "w_up", "conv_w", "conv_b", "w_down"]


def kernel(**inputs):
    n = 8
    xp = np.asarray(inputs["x_prompt"], dtype=np.float32)
    xs = np.asarray(inputs["x_sample"], dtype=np.float32)
    B, S, _ = xp.shape
    DB, DS, _ = xs.shape
    pp = B // n
    sp = DB // n
    seq_lens = [S] * pp + [DS] * sp
    depth = int(np.asarray(inputs["norm_mix"]).shape[0])
    nc = build_program(seq_lens, depth)
    tabs, cm, masks = _consts()
    shared = {k: np.ascontiguousarray(np.asarray(inputs[k], dtype=np.float32)) for k in WEIGHT_KEYS}
    shared["tabs"] = tabs
    shared["cmats"] = cm
    shared["ident"] = np.eye(128, dtype=np.float32)
    shared["masks"] = masks
    in_maps = []
    for c in range(n):
        xa = np.concatenate([xp[c * pp:(c + 1) * pp].reshape(-1, D), xs[c * sp:(c + 1) * sp].reshape(-1, D)], axis=0)
        m = dict(shared)
        m["x_all"] = np.ascontiguousarray(xa)
        in_maps.append(m)
    res = run_bass_kernel_spmd(nc, in_maps, core_ids=list(range(n)))
    yp = np.empty((B, S, D), np.float32)
    ys = np.empty((DB, DS, D), np.float32)
    for c in range(n):
        y = np.asarray(res.results[c]["y_all"], dtype=np.float32)
        yp[c * pp:(c + 1) * pp] = y[:pp * S].reshape(pp, S, D)
        ys[c * sp:(c + 1) * sp] = y[pp * S:].reshape(sp, DS, D)
    return (yp, ys)
```
